# Optimizing a Trainium2 kernel written in Bass

```python
import jax
import jax.numpy as jnp
from jax import lax
import numpy as np


D_MODEL = 1024
BATCH = 8
SEQ = 4096
DEPTH = 2

HEAD_DIM = 64
N_HEADS_A = 4
N_HEADS_C = 4
N_HEADS_D = 4
CONV_CH = 4 * HEAD_DIM
CONV_K = 3
MIX_WIDTH = (N_HEADS_A + N_HEADS_C + N_HEADS_D) * HEAD_DIM + CONV_CH
N_GROUPS = MIX_WIDTH // HEAD_DIM
BLOCK = 128
CMP_LEN = 32
CMP_STRIDE = 16
CMP_HIDDEN = 256
SEL_BLOCK = 64
N_SELECT = 16
NSA_WINDOW = 512
N_NSA_BRANCH = 3
DILATED_CONFIGS = ((128, 1), (512, 4), (2048, 16))
D_FF = -(-(8 * D_MODEL) // (3 * 256)) * 256
A_COLS = N_HEADS_A * HEAD_DIM + 6 * HEAD_DIM + N_NSA_BRANCH * N_HEADS_A
IN_COLS = A_COLS + 3 * CONV_CH + 3 * N_HEADS_C * HEAD_DIM + 3 * N_HEADS_D * HEAD_DIM
COL_WIDTHS = (N_HEADS_A * HEAD_DIM, HEAD_DIM, HEAD_DIM, HEAD_DIM, HEAD_DIM, HEAD_DIM, HEAD_DIM,
              N_NSA_BRANCH * N_HEADS_A,
              CONV_CH, CONV_CH, CONV_CH,
              N_HEADS_C * HEAD_DIM, N_HEADS_C * HEAD_DIM, N_HEADS_C * HEAD_DIM,
              N_HEADS_D * HEAD_DIM, N_HEADS_D * HEAD_DIM, N_HEADS_D * HEAD_DIM)
NEG_INF = -1e30
FORCE_SCORE = 1e6
RMS_EPS = 1e-6

kernel_name = 'hybrid_nsa_conv_stickbreak_dilated'


def rms_norm(x, g):
    xf = x.astype(jnp.float32)
    y = xf * lax.rsqrt(jnp.mean(xf * xf, axis=-1, keepdims=True) + RMS_EPS)
    return (y * g.astype(jnp.float32)).astype(x.dtype)


def to_heads(t, n):
    b, s, _ = t.shape
    return t.reshape(b, s, n, HEAD_DIM).transpose(0, 2, 1, 3)


def alibi_slopes(n):
    return jnp.exp2(-8.0 * jnp.arange(1, n + 1, dtype=jnp.float32) / n)


def banded_attention(q, k, v, max_back, dist_scale, slopes):
    n, h, l, hd = q.shape
    g = k.shape[1]
    r = h // g
    nb = l // BLOCK
    n_prev = -(-max_back // BLOCK)
    w = (n_prev + 1) * BLOCK
    qb = q.reshape(n, g, r, nb, BLOCK, hd)

    def band(t):
        tb = t.reshape(n, g, nb, BLOCK, hd)
        tp = jnp.pad(tb, ((0, 0), (0, 0), (n_prev, 0), (0, 0), (0, 0)))
        return jnp.concatenate([tp[:, :, i:i + nb] for i in range(n_prev + 1)], axis=3)

    kb, vb = band(k), band(v)
    rows = jnp.arange(BLOCK)
    diff = rows[:, None] + n_prev * BLOCK - jnp.arange(w)[None, :]
    key_idx = jnp.arange(nb)[:, None, None] * BLOCK + rows[None, :, None] - diff[None]
    mask = (diff >= 0) & (diff <= max_back) & (key_idx >= 0)
    s = jnp.einsum('ngrbqd,ngbkd->ngrbqk', qb, kb).astype(jnp.float32) * (HEAD_DIM ** -0.5)
    s = s - slopes.reshape(g, r)[None, :, :, None, None, None] * (diff * dist_scale).astype(jnp.float32)
    s = jnp.where(mask, s, NEG_INF)
    lse = jax.nn.logsumexp(s, axis=-1)
    p = jnp.exp(s - lse[..., None])
    o = jnp.einsum('ngrbqk,ngbkd->ngrbqd', p.astype(v.dtype), vb)
    return o.reshape(n, h, l, hd), lse.reshape(n, h, l)


def compress_tokens(t, pe, w1, w2):
    b, s, _ = t.shape
    n_cmp = (s - CMP_LEN) // CMP_STRIDE + 1
    idx = jnp.arange(n_cmp)[:, None] * CMP_STRIDE + jnp.arange(CMP_LEN)[None, :]
    blocks = (t[:, idx] + pe).reshape(b, n_cmp, CMP_LEN * HEAD_DIM)
    return jax.nn.gelu(blocks @ w1) @ w2


def nsa_mixer(q_tok, kc_tok, vc_tok, ks_tok, vs_tok, kw_tok, vw_tok, gate_logits, b_gate,
              g_q, g_kc, g_ks, g_kw, pe_k, pe_v, w1_k, w2_k, w1_v, w2_v, slopes):
    b, s, _ = q_tok.shape
    h = N_HEADS_A
    scale = HEAD_DIM ** -0.5
    q = rms_norm(to_heads(q_tok, h), g_q)
    kc = rms_norm(compress_tokens(kc_tok, pe_k, w1_k, w2_k), g_kc)
    vc = compress_tokens(vc_tok, pe_v, w1_v, w2_v)
    n_cmp = kc.shape[1]
    cmp_end = jnp.arange(n_cmp) * CMP_STRIDE + CMP_LEN - 1
    n_sel = s // SEL_BLOCK
    top = min(N_SELECT, n_sel)
    ratio = SEL_BLOCK // CMP_STRIDE
    ks = rms_norm(ks_tok, g_ks).reshape(b, n_sel, SEL_BLOCK, HEAD_DIM)
    vs = vs_tok.reshape(b, n_sel, SEL_BLOCK, HEAD_DIM)
    blk_ids = jnp.arange(n_sel)
    bidx = jnp.arange(b)[:, None, None]
    nc = s // BLOCK
    q_chunks = q.reshape(b, h, nc, BLOCK, HEAD_DIM).transpose(2, 0, 1, 3, 4)

    def chunk(args):
        qi, c = args
        t = c * BLOCK + jnp.arange(BLOCK)
        dist_c = t[:, None] - cmp_end[None, :]
        vis = dist_c >= 0
        sc = jnp.einsum('bhqd,bnd->bhqn', qi, kc).astype(jnp.float32) * scale
        sc = jnp.where(vis, sc - slopes[:, None, None] * dist_c.astype(jnp.float32), NEG_INF)
        p_cmp = jax.nn.softmax(sc, axis=-1) * vis
        o_cmp = jnp.einsum('bhqn,bnd->bhqd', p_cmp.astype(vc.dtype), vc)
        imp = jnp.pad(p_cmp.sum(1), ((0, 0), (0, 0), (0, ratio * n_sel - n_cmp)))
        imp_ov = imp
        for sh in range(1, CMP_LEN // CMP_STRIDE):
            imp_ov = imp_ov + jnp.pad(imp[..., :-sh], ((0, 0), (0, 0), (sh, 0)))
        imp_blk = imp_ov.reshape(b, BLOCK, n_sel, ratio).sum(-1)
        cur = t // SEL_BLOCK
        forced = (blk_ids[None] == 0) | (blk_ids[None] == cur[:, None]) | (blk_ids[None] == cur[:, None] - 1)
        valid = blk_ids[None] * SEL_BLOCK <= t[:, None]
        score = jnp.where(forced, FORCE_SCORE, jnp.where(valid, imp_blk, -FORCE_SCORE))
        _, idx = lax.top_k(score, top)
        kg = ks[bidx, idx]
        vg = vs[bidx, idx]
        pos = idx[..., None] * SEL_BLOCK + jnp.arange(SEL_BLOCK)
        dist_s = t[None, :, None, None] - pos
        ss = jnp.einsum('bhqd,bqkjd->bhqkj', qi, kg).astype(jnp.float32) * scale
        ss = ss - slopes[None, :, None, None, None] * dist_s[:, None].astype(jnp.float32)
        ss = jnp.where((dist_s >= 0)[:, None], ss, NEG_INF)
        p_slc = jax.nn.softmax(ss.reshape(b, h, BLOCK, top * SEL_BLOCK), axis=-1).reshape(ss.shape)
        o_slc = jnp.einsum('bhqkj,bqkjd->bhqd', p_slc.astype(vg.dtype), vg)
        return o_cmp, o_slc

    o_cmp, o_slc = lax.map(chunk, (q_chunks, jnp.arange(nc)))
    o_cmp = o_cmp.transpose(1, 2, 0, 3, 4).reshape(b, h, s, HEAD_DIM)
    o_slc = o_slc.transpose(1, 2, 0, 3, 4).reshape(b, h, s, HEAD_DIM)
    o_win, _ = banded_attention(q, rms_norm(kw_tok, g_kw)[:, None], vw_tok[:, None], NSA_WINDOW - 1, 1, slopes)
    gates = jax.nn.sigmoid(gate_logits + b_gate).reshape(b, s, h, N_NSA_BRANCH).transpose(0, 2, 1, 3)
    return gates[..., 0:1] * o_cmp + gates[..., 1:2] * o_slc + gates[..., 2:3] * o_win


def short_conv_mixer(gate_b, gate_c, u, conv_w):
    y = lax.conv_general_dilated(gate_c * u, conv_w[:, None, :], window_strides=(1,),
                                 padding=[(CONV_K - 1, 0)], dimension_numbers=('NWC', 'WIO', 'NWC'),
                                 feature_group_count=CONV_CH)
    return gate_b * y


def stick_breaking_attention(q, k, v):
    b, h, s, hd = q.shape
    nc = s // BLOCK
    q_chunks = q.reshape(b, h, nc, BLOCK, hd).transpose(2, 0, 1, 3, 4)
    key_pos = jnp.arange(s)

    def chunk(args):
        qi, c = args
        t = c * BLOCK + jnp.arange(BLOCK)
        z = jnp.einsum('bhqd,bhsd->bhqs', qi, k).astype(jnp.float32) * (hd ** -0.5)
        past = key_pos[None, :] < t[:, None]
        neg_log_keep = jnp.where(past, jax.nn.softplus(z), 0.0)
        between = lax.cumsum(neg_log_keep, axis=3, reverse=True) - neg_log_keep
        attn = jnp.where(past, jnp.exp(jax.nn.log_sigmoid(z) - between), 0.0)
        return jnp.einsum('bhqs,bhsd->bhqd', attn.astype(v.dtype), v)

    o = lax.map(chunk, (q_chunks, jnp.arange(nc)))
    return o.transpose(1, 2, 0, 3, 4).reshape(b, h, s, hd)


def dilated_attention(q, k, v, window, dil, slopes):
    b, h, s, hd = q.shape
    span = dil * BLOCK
    sp = -(-s // span) * span
    l = sp // dil

    def fold(t):
        t = jnp.pad(t, ((0, 0), (0, 0), (0, sp - s), (0, 0)))
        return t.reshape(b, h, l, dil, hd).transpose(0, 3, 1, 2, 4).reshape(b * dil, h, l, hd)

    o, lse = banded_attention(fold(q), fold(k), fold(v), window // dil, dil, slopes)
    o = o.reshape(b, dil, h, l, hd).transpose(0, 2, 3, 1, 4).reshape(b, h, sp, hd)[:, :, :s]
    lse = lse.reshape(b, dil, h, l).transpose(0, 2, 3, 1).reshape(b, h, sp)[:, :, :s]
    return o, lse


def hybrid_layer(x, g_mix, w_in, b_gate, g_q_nsa, g_k_cmp, g_k_slc, g_k_win, pe_k_cmp, pe_v_cmp,
                 w1_k_cmp, w2_k_cmp, w1_v_cmp, w2_v_cmp, conv_w, g_q_dil, g_k_dil, g_out, w_out,
                 g_ffn, w_gate, w_up, w_down):
    b, s, _ = x.shape
    hn = rms_norm(x, g_mix)
    proj = hn @ w_in
    cuts = [int(c) for c in np.cumsum(COL_WIDTHS)[:-1]]
    (qa, kca, vca, ksa, vsa, kwa, vwa, gta, cvb, cvc, cvu,
     qc, kc, vc, qd, kd, vd) = jnp.split(proj, cuts, axis=-1)
    slopes = alibi_slopes(N_HEADS_A + N_HEADS_D)
    slopes_a, slopes_d = slopes[0::2], slopes[1::2]
    o_a = nsa_mixer(qa, kca, vca, ksa, vsa, kwa, vwa, gta, b_gate, g_q_nsa, g_k_cmp, g_k_slc, g_k_win,
                    pe_k_cmp, pe_v_cmp, w1_k_cmp, w2_k_cmp, w1_v_cmp, w2_v_cmp, slopes_a)
    o_b = short_conv_mixer(cvb, cvc, cvu, conv_w).reshape(b, s, CONV_CH // HEAD_DIM, HEAD_DIM)
    o_c = stick_breaking_attention(to_heads(qc, N_HEADS_C), to_heads(kc, N_HEADS_C), to_heads(vc, N_HEADS_C))
    qdh = rms_norm(to_heads(qd, N_HEADS_D), g_q_dil)
    kdh = rms_norm(to_heads(kd, N_HEADS_D), g_k_dil)
    vdh = to_heads(vd, N_HEADS_D)
    outs, lses = [], []
    for window, dil in DILATED_CONFIGS:
        o_i, lse_i = dilated_attention(qdh, kdh, vdh, window, dil, slopes_d)
        outs.append(o_i)
        lses.append(lse_i)
    mix_w = jax.nn.softmax(jnp.stack(lses, axis=0), axis=0)
    o_d = jnp.sum(mix_w[..., None].astype(vdh.dtype) * jnp.stack(outs, axis=0), axis=0)
    groups = jnp.concatenate([o_a.transpose(0, 2, 1, 3), o_b, o_c.transpose(0, 2, 1, 3),
                              o_d.transpose(0, 2, 1, 3)], axis=2)
    groups = rms_norm(groups, g_out.reshape(N_GROUPS, HEAD_DIM)).reshape(b, s, MIX_WIDTH)
    x = x + groups @ w_out
    h2 = rms_norm(x, g_ffn)
    return x + (jax.nn.silu(h2 @ w_gate) * (h2 @ w_up)) @ w_down


def setup_inputs(seed: int = 0) -> dict:
    key = jax.random.key(seed)
    ks = jax.random.split(key, 23)
    L = DEPTH
    hd = HEAD_DIM

    def nrm(k, shape, scale):
        return jax.random.normal(k, shape, jnp.float32) * scale

    def gain(k, shape):
        return 1.0 + 0.02 * jax.random.normal(k, shape, jnp.float32)

    return {
        'x': nrm(ks[0], (BATCH, SEQ, D_MODEL), 1.0),
        'g_mix': gain(ks[1], (L, D_MODEL)),
        'w_in': nrm(ks[2], (L, D_MODEL, IN_COLS), D_MODEL ** -0.5),
        'b_gate': nrm(ks[3], (L, N_NSA_BRANCH * N_HEADS_A), 0.1),
        'g_q_nsa': gain(ks[4], (L, hd)),
        'g_k_cmp': gain(ks[5], (L, hd)),
        'g_k_slc': gain(ks[6], (L, hd)),
        'g_k_win': gain(ks[7], (L, hd)),
        'pe_k_cmp': nrm(ks[8], (L, CMP_LEN, hd), 0.1),
        'pe_v_cmp': nrm(ks[9], (L, CMP_LEN, hd), 0.1),
        'w1_k_cmp': nrm(ks[10], (L, CMP_LEN * hd, CMP_HIDDEN), (CMP_LEN * hd) ** -0.5),
        'w2_k_cmp': nrm(ks[11], (L, CMP_HIDDEN, hd), CMP_HIDDEN ** -0.5),
        'w1_v_cmp': nrm(ks[12], (L, CMP_LEN * hd, CMP_HIDDEN), (CMP_LEN * hd) ** -0.5),
        'w2_v_cmp': nrm(ks[13], (L, CMP_HIDDEN, hd), CMP_HIDDEN ** -0.5),
        'conv_w': nrm(ks[14], (L, CONV_K, CONV_CH), CONV_K ** -0.5),
        'g_q_dil': gain(ks[15], (L, hd)),
        'g_k_dil': gain(ks[16], (L, hd)),
        'g_out': gain(ks[17], (L, MIX_WIDTH)),
        'w_out': nrm(ks[18], (L, MIX_WIDTH, D_MODEL), MIX_WIDTH ** -0.5),
        'g_ffn': gain(ks[19], (L, D_MODEL)),
        'w_gate': nrm(ks[20], (L, D_MODEL, D_FF), D_MODEL ** -0.5),
        'w_up': nrm(ks[21], (L, D_MODEL, D_FF), D_MODEL ** -0.5),
        'w_down': nrm(ks[22], (L, D_FF, D_MODEL), D_FF ** -0.5),
    }


def reference(x, g_mix, w_in, b_gate, g_q_nsa, g_k_cmp, g_k_slc, g_k_win, pe_k_cmp, pe_v_cmp,
              w1_k_cmp, w2_k_cmp, w1_v_cmp, w2_v_cmp, conv_w, g_q_dil, g_k_dil, g_out, w_out,
              g_ffn, w_gate, w_up, w_down):
    for l in range(DEPTH):
        x = hybrid_layer(x, g_mix[l], w_in[l], b_gate[l], g_q_nsa[l], g_k_cmp[l], g_k_slc[l], g_k_win[l],
                         pe_k_cmp[l], pe_v_cmp[l], w1_k_cmp[l], w2_k_cmp[l], w1_v_cmp[l], w2_v_cmp[l],
                         conv_w[l], g_q_dil[l], g_k_dil[l], g_out[l], w_out[l], g_ffn[l],
                         w_gate[l], w_up[l], w_down[l])
    return x
```

```python
import numpy as np
from contextlib import ExitStack
import concourse.bass as bass
import concourse.mybir as mybir
from concourse.bass_utils import run_bass_kernel_spmd

F32 = mybir.dt.float32
BF16 = mybir.dt.bfloat16
U8 = mybir.dt.uint8
AF = mybir.ActivationFunctionType
ALU = mybir.AluOpType
AX = mybir.AxisListType

S = 4096
D = 1024
NB = S // 128
HD = 64
DEPTH = 2
IN_COLS = 2956
D_FF = 2816
NFC = D_FF // 128
N_CMP = 255
EPS = 1e-6
BIG = 30000.0
SLOPES = [2.0 ** (-(i + 1)) for i in range(8)]
SLOPES_A = SLOPES[0::2]
SLOPES_D = SLOPES[1::2]
DIL_CFG = ((128, 1), (512, 4), (2048, 16))

EPOCH = 8000
DBG_NQ = NB
D_L = 1
C_FILL = 3
B_FILL = 2
NDMA_SLOTS = 32


class Op:
    __slots__ = ("eng", "fn", "deps", "is_dma", "slot", "slot_val", "slot_prev", "need_inc")


class Prog:
    ENGS = ("pe", "act", "dve", "pool", "sp")

    def __init__(self, nc):
        self.nc = nc
        self.ops = []
        self.last_w = {}
        self.readers = {}
        self.streams = {e: [] for e in self.ENGS}
        self.dma_cnt = {}
        self.pending_fence = {}
        self.dma_since_fence = []

    def op(self, eng, fn, reads=(), writes=(), dma=False):
        o = Op()
        o.eng, o.fn, o.is_dma = eng, fn, dma
        deps = set()
        for k in reads:
            w = self.last_w.get(k)
            if w is not None:
                deps.add(w)
        for k in writes:
            w = self.last_w.get(k)
            if w is not None:
                deps.add(w)
            for r in self.readers.get(k, ()):
                deps.add(r)
        f = self.pending_fence.pop(eng, None)
        if f:
            deps |= f
        gi = len(self.ops)
        o.deps = deps
        o.need_inc = False
        o.slot = None
        self.ops.append(o)
        self.streams[eng].append(gi)
        for k in reads:
            self.readers.setdefault(k, []).append(gi)
        for k in writes:
            self.last_w[k] = gi
            self.readers[k] = []
        if dma:
            half = NDMA_SLOTS // 2
            c = self.dma_cnt.get(eng, 0)
            self.dma_cnt[eng] = c + 1
            o.slot = (c % half) + (half if eng == "pool" else 0)
            self.dma_since_fence.append(gi)
        return gi

    def fence(self):
        f = set(self.dma_since_fence)
        for e in self.ENGS:
            if self.streams[e]:
                f.add(self.streams[e][-1])
        for e in self.ENGS:
            prev = self.pending_fence.get(e)
            self.pending_fence[e] = set(f) | (prev or set())
        self.dma_since_fence = []
        self.last_w = {}
        self.readers = {}

    def emit(self, stack):
        nc = self.nc
        ops = self.ops
        for o in ops:
            for d in o.deps:
                od = ops[d]
                if od.is_dma:
                    continue
                if od.eng == "pe" and o.eng == "pe" and not o.is_dma:
                    continue
                od.need_inc = True
        cnt = {e: 0 for e in self.ENGS}
        val = {}
        for gi, o in enumerate(ops):
            if o.is_dma:
                continue
            if o.need_inc:
                cnt[o.eng] += 1
                val[gi] = cnt[o.eng]
        nep = {e: (cnt[e] + EPOCH - 1) // EPOCH + 1 for e in self.ENGS}
        sems = {e: [stack.enter_context(nc.semaphore(f"s_{e}_{i}")) for i in range(nep[e])] for e in self.ENGS}
        dsems = [stack.enter_context(nc.semaphore(f"s_dma_{i}")) for i in range(NDMA_SLOTS)]
        slot_tot = [0] * NDMA_SLOTS
        slot_last = [None] * NDMA_SLOTS
        for gi, o in enumerate(ops):
            if o.is_dma:
                o.slot_prev = slot_last[o.slot]
                slot_tot[o.slot] += 16
                o.slot_val = slot_tot[o.slot]
                slot_last[o.slot] = gi

        def sem_of(gi):
            o = ops[gi]
            if o.is_dma:
                return dsems[o.slot], o.slot_val, ("d", o.slot)
            v = val[gi] - 1
            return sems[o.eng][v // EPOCH], v % EPOCH + 1, (o.eng, v // EPOCH)

        def run_engine(ename, eng):
            waited = {}
            for gi in self.streams[ename]:
                o = ops[gi]
                deps = set(o.deps)
                if o.is_dma and o.slot_prev is not None:
                    deps.add(o.slot_prev)
                need = {}
                for d in deps:
                    od = ops[d]
                    if (not od.is_dma) and od.eng == "pe" and ename == "pe" and not o.is_dma:
                        continue
                    s, v, key = sem_of(d)
                    if key not in need or need[key][1] < v:
                        need[key] = (s, v)
                for key, (s, v) in need.items():
                    if waited.get(key, 0) >= v:
                        continue
                    eng.wait_ge(s, v)
                    waited[key] = v
                ins = o.fn(eng)
                if o.is_dma:
                    ins.then_inc(dsems[o.slot], 16)
                elif o.need_inc:
                    s, v, key = sem_of(gi)
                    ins.then_inc(s, 1)
            if ename in ("sp", "pool"):
                lastv = {}
                for gi in self.streams[ename]:
                    o = ops[gi]
                    if o.is_dma:
                        lastv[o.slot] = max(lastv.get(o.slot, 0), o.slot_val)
                for sl, v in lastv.items():
                    eng.wait_ge(dsems[sl], v)

        with nc.Block() as block:
            @block.tensor
            def _(e):
                run_engine("pe", e)

            @block.scalar
            def _(e):
                run_engine("act", e)

            @block.vector
            def _(e):
                run_engine("dve", e)

            @block.gpsimd
            def _(e):
                run_engine("pool", e)

            @block.sync
            def _(e):
                run_engine("sp", e)


def host_consts():
    c = {}
    i = np.arange(128)
    c["ident"] = np.eye(128, dtype=np.float32)
    c["tri_incl"] = (i[:, None] >= i[None, :]).astype(np.float32)
    c["tri_rest"] = 1.0 - c["tri_incl"]
    ob = np.zeros((128, 128), np.float32)
    ob[:64, :64] = 1.0
    ob[64:, 64:] = 1.0
    c["ones_blk"] = ob
    kl = i[:, None]
    ql = i[None, :]
    rep4 = lambda m: np.tile(m.astype(np.float32), (1, 4))
    masks = [rep4(kl <= ql), rep4(kl < ql), rep4(ql < kl)]
    dil_tiles = []
    dil_idx = []
    for dl in range(17):
        d = 128 * dl + ql - kl
        m = np.zeros((128, 128), np.float32)
        for window, dil in DIL_CFG:
            m += ((d >= 0) & (d <= window) & (d % dil == 0)).astype(np.float32)
        found = None
        for ti, t in enumerate(dil_tiles):
            if np.array_equal(t, m):
                found = ti
        if found is None:
            dil_tiles.append(m)
            found = len(dil_tiles) - 1
        dil_idx.append(found)
    c["_dil_idx"] = dil_idx
    c["_n_dil"] = len(dil_tiles)
    masks += [rep4(t) for t in dil_tiles]
    c["masks"] = np.ascontiguousarray(np.stack(masks, 0))
    t = np.arange(S)
    a_t, b_t = t // 64, t % 64
    alk = np.stack([-np.ones(S), -np.ones(S), a_t, b_t], 0).astype(np.float32)
    c["al_k"] = alk

    def qrows(slopes):
        r = np.zeros((4, 4, S), np.float32)
        for h, s in enumerate(slopes):
            r[0, h] = 8 * s * 64 * a_t
            r[1, h] = 8 * s * b_t
            r[2, h] = 8 * s * 64
            r[3, h] = 8 * s
        return r
    qa = qrows(SLOPES_A)
    c["al_qa"] = np.ascontiguousarray(qa.reshape(4, 4, NB, 128).transpose(0, 2, 1, 3).reshape(4, NB * 512))
    c["al_qd"] = np.ascontiguousarray(qrows(SLOPES_D).reshape(4, 4, NB, 128).transpose(0, 2, 1, 3).reshape(4, NB * 512))
    jj = np.arange(504)
    dd = (i[:, None] - 16 * (jj[None, :] - 248) - 31).astype(np.float64)
    cb = np.zeros((128, 4, 504), np.float32)
    for h, sl in enumerate(SLOPES_A):
        cb[:, h, :] = np.where(dd >= 0, -sl * dd, 1.0e4 * dd)
    c["cmp_bias"] = cb.reshape(128, 4 * 504)
    fb = np.zeros((128, NB, 64), np.float32)
    j = np.arange(64)
    for qb in range(NB):
        tt = qb * 128 + i
        cur = tt // 64
        forced = (j[None] == 0) | (j[None] == cur[:, None]) | (j[None] == cur[:, None] - 1)
        valid = j[None] * 64 <= tt[:, None]
        fb[:, qb, :] = np.where(forced, 1e6, np.where(valid, 0.0, -1e6))
    c["sel_fb"] = fb.reshape(128, NB * 64)
    e = np.zeros((64, S), np.float32)
    e[t // 64, t] = 1.0
    c["sel_e"] = e
    return c


CONST_NAMES = ["ident", "tri_incl", "tri_rest", "ones_blk", "masks", "al_k", "al_qa", "al_qd", "cmp_bias", "sel_fb", "sel_e"]
PARAM_NAMES = ["g_mix", "w_in", "b_gate", "g_q_nsa", "g_k_cmp", "g_k_slc", "g_k_win", "pe_k_cmp", "pe_v_cmp",
               "w1_k_cmp", "w2_k_cmp", "w1_v_cmp", "w2_v_cmp", "conv_w", "g_q_dil", "g_k_dil", "g_out", "w_out",
               "g_ffn", "w_gate", "w_up", "w_down"]


class KB:
    def __init__(self, nc, stack, shapes, dbg=None):
        self.nc = nc
        self.P = Prog(nc)
        self.dbg = dbg or {}
        self.I = {}
        for name, shp in shapes.items():
            self.I[name] = nc.dram_tensor(name, list(shp), F32, kind="ExternalInput").ap()
        self.y = nc.dram_tensor("y", [S, D], F32, kind="ExternalOutput").ap()
        ARENA = 204 * 1024
        self.arena_t = stack.enter_context(nc.sbuf_tensor("arena", [128, ARENA], U8))
        self.arena_size = ARENA
        self.off = 0
        self.banks = [stack.enter_context(nc.psum_tensor(f"bank{i}", [128, 512], F32)) for i in range(8)]
        self.rot = 0
        self.rot_banks = list(range(8))
        self.uid = 0

    def dram(self, name, shape, dt):
        kind = "ExternalOutput" if name in self.dbg else "Internal"
        return self.nc.dram_tensor(name, list(shape), dt, kind=kind).ap()

    def sb(self, free_shape, dt, parts=128):
        n = int(np.prod(free_shape))
        esz = {F32: 4, BF16: 2, U8: 1}[dt]
        nbytes = n * esz
        self.off = (self.off + 63) // 64 * 64
        assert self.off + nbytes <= self.arena_size, f"arena overflow {self.off + nbytes}"
        ap = self.arena_t[0:parts, self.off:self.off + nbytes]
        if dt != U8:
            ap = ap.bitcast(dt)
        self.off += nbytes
        if len(free_shape) == 2:
            ap = ap.rearrange("p (a b) -> p a b", a=free_shape[0], b=free_shape[1])
        elif len(free_shape) == 3:
            ap = ap.rearrange("p (a b c) -> p a b c", a=free_shape[0], b=free_shape[1], c=free_shape[2])
        return ap

    def key(self, base):
        self.uid += 1
        return f"{base}#{self.uid}"

    def next_bank(self):
        b = self.rot_banks[self.rot % len(self.rot_banks)]
        self.rot += 1
        return b

    def new_phase(self, rot_banks=None, reserve=0):
        self.P.fence()
        self.off = self.const_end + reserve
        self.rot_banks = rot_banks or list(range(8))
        self.rot = 0

    def dma(self, out, in_, reads=(), writes=(), eng="sp", **kw):
        return self.P.op(eng, lambda e: e.dma_start(out=out, in_=in_, **kw), reads, writes, dma=True)

    def dma_cast(self, out, in_, reads=(), writes=(), **kw):
        return self.P.op("pool", lambda e: e.dma_start(out=out, in_=in_, max_dma_last_dim=4096, **kw), reads, writes, dma=True)

    def mm(self, out, lhsT, rhs, start, stop, reads, writes, **kw):
        return self.P.op("pe", lambda e: e.matmul(out, lhsT=lhsT, rhs=rhs, start=start, stop=stop, **kw), reads, writes)

    def tr(self, out, in_, ident, reads, writes):
        return self.P.op("pe", lambda e: e.transpose(out=out, in_=in_, identity=ident), reads, writes)

    def act(self, out, in_, func, reads, writes, **kw):
        return self.P.op("act", lambda e: e.activation(out=out, in_=in_, func=func, **kw), reads, writes)

    def v(self, eng, method, reads, writes, *a, **kw):
        return self.P.op(eng, lambda e: getattr(e, method)(*a, **kw), reads, writes)

    def load_consts(self, cinfo):
        I = self.I
        self.off = 0
        self.identb = self.sb([128], BF16)
        self.identf = self.sb([128], F32)
        self.tri_incl = self.sb([128], BF16)
        self.tri_rest = self.sb([128], BF16)
        self.ones_blk = self.sb([128], F32)
        nm = cinfo["nm"]
        self.masks = self.sb([nm, 512], BF16)
        self.dma_cast(self.identb, I["ident"], writes=["identb"])
        self.dma(self.identf, I["ident"], writes=["identf"])
        self.dma_cast(self.tri_incl, I["tri_incl"], writes=["tri_incl"])
        self.dma_cast(self.tri_rest, I["tri_rest"], writes=["tri_rest"])
        self.dma(self.ones_blk, I["ones_blk"], writes=["ones_blk"])
        for m in range(nm):
            self.dma_cast(self.masks[:, m, :], I["masks"][m], writes=["masks"])
        self.const_end = self.off
        self.dil_idx = cinfo["dil_idx"]


def run_pipeline(items, L):
    n = len(items)
    for i in range(n + L):
        if i < n and items[i][0] is not None:
            items[i][0]()
        j = i - L
        if j >= 0 and items[j][1] is not None:
            items[j][1]()


def rms_rows(k, x_t, gm32, hn_out, ss, rk, wk, junk):
    k.act(junk, x_t, AF.Square, rk, ["junk", "ss"], accum_out=ss)
    k.act(ss, ss, AF.Sqrt, ["ss"], ["ss"], bias=float(D * EPS))
    k.v("dve", "reciprocal", ["ss"], ["ss"], out=ss, in_=ss)
    k.v("dve", "scalar_tensor_tensor", list(rk) + ["ss", "gm32"], wk, out=hn_out, in0=x_t, scalar=ss, in1=gm32,
        op0=ALU.mult, op1=ALU.mult)


def phase_A(k, l, x_src, T):
    I, P = k.I, k.P
    k.new_phase()
    w = k.sb([8, IN_COLS], BF16)
    for kc in range(8):
        k.dma_cast(w[:, kc, :], I["w_in"][l, kc * 128:(kc + 1) * 128, :], writes=[("w", kc)])
    wkeys = [("w", kc) for kc in range(8)]
    gm32 = k.sb([D], F32)
    k.dma(gm32, I["g_mix"][l:l + 1, :].partition_broadcast(128), writes=["gm32"])
    k.v("dve", "tensor_scalar_mul", ["gm32"], ["gm32"], out=gm32, in0=gm32, scalar1=32.0)
    gcol = k.sb([6], F32)
    col = lambda ap: ap.rearrange("a (d o) -> (a d) o", o=1)
    for (c, nm) in ((0, "g_q_nsa"), (2, "g_q_dil"), (3, "g_k_dil")):
        for hf in range(2):
            k.dma(gcol[hf * 64:(hf + 1) * 64, c:c + 1], col(I[nm][l:l + 1, :]), writes=["gcol"])
    k.dma(gcol[0:64, 1:2], col(I["g_k_slc"][l:l + 1, :]), writes=["gcol"])
    k.dma(gcol[64:128, 1:2], col(I["g_k_win"][l:l + 1, :]), writes=["gcol"])
    for hh in range(2):
        k.dma(gcol[:, 4 + hh:5 + hh], col(I["g_out"][l:l + 1, 256 + hh * 128:256 + (hh + 1) * 128]), writes=["gcol"])
    k.v("dve", "tensor_scalar_mul", ["gcol"], ["gcol"], out=gcol, in0=gcol, scalar1=8.0)
    cw = k.sb([2, 3], F32)
    for hh in range(2):
        k.dma(cw[:, hh, :], I["conv_w"][l, :, hh * 128:(hh + 1) * 128].rearrange("k p -> p k"), writes=["cw"],
              allow_slow_non_contiguous=True)
    bg = k.sb([12], F32)
    k.dma(bg, I["b_gate"][l:l + 1, :].partition_broadcast(128), writes=["bg"])

    xt = [k.sb([D], F32) for _ in range(2)]
    junk = k.sb([D], F32)
    ssv = k.sb([1], F32)
    hn = [k.sb([D], BF16) for _ in range(2)]
    hnT = [k.sb([8, 512], BF16) for _ in range(2)]
    sq = [k.sb([512], F32) for _ in range(2)]
    rr = [k.sb([512], F32) for _ in range(2)]
    ob16 = [k.sb([512], BF16) for _ in range(3)]
    tmo = [k.sb([640], BF16) for _ in range(2)]
    gts = [k.sb([12], F32) for _ in range(2)]
    cu = [k.sb([514], F32) for _ in range(2)]
    csb = k.sb([512], F32)
    yv = k.sb([512], F32)
    for hh in range(2):
        k.v("pool", "memset", [], [("cu", hh)], cu[hh], 0.0)

    def wcols(c0, n=128):
        return lambda kc: w[:, kc, c0:c0 + n]

    def wcols_ks_kw(kc):
        return w[:, kc, 384:640].rearrange("p (a b) -> p a b", b=128)[:, :, 0:64]
    fblocks = [
        (wcols(0), "norm", 0, 0), (wcols(128), "norm", 0, 128), (wcols(256), "raw", None, 256),
        ("kskw", "norm", 1, 384),
        (wcols(1420), "raw", None, 512), (wcols(1548), "raw", None, 640),
        (wcols(1676), "raw", None, 768), (wcols(1804), "raw", None, 896),
        (wcols(2188), "norm", 2, 1024), (wcols(2316), "norm", 2, 1152),
        (wcols(2444), "norm", 3, 1280), (wcols(2572), "norm", 3, 1408),
    ]
    ft, tm, gates, gt = T["ft"], T["tm"], T["gates"], T["gt"]
    nob = [0]

    def stage1(g):
        hT = hnT[g % 2]
        hk = ("hnT", g % 2)
        for tt in range(4):
            tok0 = g * 512 + tt * 128
            ti = (g * 4 + tt) % 2
            k.dma(xt[ti], x_src[tok0:tok0 + 128, :], writes=[("xt", ti)])
            rms_rows(k, xt[ti], gm32, hn[ti], ssv, [("xt", ti)], [("hn", ti)], junk)
            b = k.next_bank()
            pb = k.banks[b][:, :].bitcast(BF16)
            for kc in range(8):
                k.tr(pb[:, kc * 128:(kc + 1) * 128], hn[ti][:, kc * 128:(kc + 1) * 128], k.identb,
                     [("hn", ti), "identb"], [("bank", b)])
            k.v("dve", "tensor_copy", [("bank", b)], [hk], out=hT[:, :, tt * 128:(tt + 1) * 128],
                in_=pb.rearrange("p (a b) -> p a b", a=8))
            to = tmo[ti]
            for (c0, n, dst0) in ((448, 204, None), (1932, 256, 128), (2700, 256, 384)):
                b2 = k.next_bank()
                ps = k.banks[b2]
                for kc in range(8):
                    k.mm(ps[:, 0:n], hT[:, kc, tt * 128:(tt + 1) * 128], w[:, kc, c0:c0 + n], kc == 0, kc == 7,
                         [hk] + wkeys, [("bank", b2)])
                if dst0 is None:
                    k.v("dve", "tensor_copy", [("bank", b2)], [("tmo", ti)], out=to[:, 0:64], in_=ps[:, 0:64])
                    k.v("dve", "tensor_copy", [("bank", b2)], [("tmo", ti)], out=to[:, 64:128], in_=ps[:, 128:192])
                    k.v("dve", "tensor_tensor", [("bank", b2), "bg"], [("gts", ti)], out=gts[ti], in0=ps[:, 192:204],
                        in1=bg, op=ALU.add)
                    k.act(gts[ti], gts[ti], AF.Sigmoid, [("gts", ti)], [("gts", ti)])
                    k.dma(gates[tok0:tok0 + 128, :], gts[ti], reads=[("gts", ti)])
                else:
                    k.act(to[:, dst0:dst0 + 256], ps[:, 0:256], AF.Copy, [("bank", b2)], [("tmo", ti)])
            k.dma(tm[tok0:tok0 + 128, :], to, reads=[("tmo", ti)])

    def stage2(g):
        hT = hnT[g % 2]
        hk = ("hnT", g % 2)
        tsl = slice(g * 512, (g + 1) * 512)
        for (wc, kind, gc, r0) in fblocks:
            b = k.next_bank()
            ps = k.banks[b]
            if wc == "kskw":
                for (p0, c0) in ((0, 384), (64, 512)):
                    for kc in range(8):
                        k.mm(ps[p0:p0 + 64, :], w[:, kc, c0:c0 + 64], hT[:, kc, :], kc == 0, kc == 7, [hk] + wkeys,
                             [("bank", b)])
            else:
                for kc in range(8):
                    k.mm(ps[:, :], wc(kc), hT[:, kc, :], kc == 0, kc == 7, [hk] + wkeys, [("bank", b)])
            oi = nob[0] % 3
            nob[0] += 1
            if kind == "raw":
                k.act(ob16[oi], ps[:, :], AF.Copy, [("bank", b)], [("ob16", oi)])
            else:
                si = nob[0] % 2
                k.act(sq[si], ps[:, :], AF.Square, [("bank", b)], [("sq", si)])
                b3 = k.next_bank()
                k.mm(k.banks[b3][:, :], k.ones_blk, sq[si], True, True, [("sq", si), "ones_blk"], [("bank", b3)])
                k.act(rr[si], k.banks[b3][:, :], AF.Sqrt, [("bank", b3)], [("rr", si)], bias=float(64 * EPS))
                k.v("dve", "reciprocal", [("rr", si)], [("rr", si)], out=rr[si], in_=rr[si])
                k.v("dve", "scalar_tensor_tensor", [("bank", b), ("rr", si), "gcol"], [("ob16", oi)], out=ob16[oi],
                    in0=ps[:, :], scalar=gcol[:, gc:gc + 1], in1=rr[si], op0=ALU.mult, op1=ALU.mult)
            k.dma(ft[r0:r0 + 128, tsl], ob16[oi], reads=[("ob16", oi)])
        for hh in range(2):
            bb, bc, bu = k.next_bank(), k.next_bank(), k.next_bank()
            for (bx, c0) in ((bb, 652), (bc, 908), (bu, 1164)):
                for kc in range(8):
                    k.mm(k.banks[bx][:, :], w[:, kc, c0 + hh * 128:c0 + (hh + 1) * 128], hT[:, kc, :], kc == 0, kc == 7,
                         [hk] + wkeys, [("bank", bx)])
            k.act(csb, k.banks[bc][:, :], AF.Copy, [("bank", bc)], ["csb"])
            k.v("dve", "tensor_tensor", ["csb", ("bank", bu)], [("cu", hh)], out=cu[hh][:, 2:514], in0=csb,
                in1=k.banks[bu][:, :], op=ALU.mult)
            k.v("dve", "tensor_scalar_mul", [("cu", hh), "cw"], ["yv"], out=yv, in0=cu[hh][:, 0:512],
                scalar1=cw[:, hh, 0:1])
            k.v("dve", "scalar_tensor_tensor", [("cu", hh), "cw", "yv"], ["yv"], out=yv, in0=cu[hh][:, 1:513],
                scalar=cw[:, hh, 1:2], in1=yv, op0=ALU.mult, op1=ALU.add)
            k.v("dve", "scalar_tensor_tensor", [("cu", hh), "cw", "yv"], ["yv"], out=yv, in0=cu[hh][:, 2:514],
                scalar=cw[:, hh, 2:3], in1=yv, op0=ALU.mult, op1=ALU.add)
            k.v("dve", "tensor_tensor", ["yv", ("bank", bb)], ["yv"], out=yv, in0=yv, in1=k.banks[bb][:, :], op=ALU.mult)
            k.v("pool", "tensor_copy", [("cu", hh), "yv"], [("cu", hh)], out=cu[hh][:, 0:2], in_=cu[hh][:, 512:514])
            si = nob[0] % 2
            oi = nob[0] % 3
            nob[0] += 1
            k.act(sq[si], yv, AF.Square, ["yv"], [("sq", si)])
            b3 = k.next_bank()
            k.mm(k.banks[b3][:, :], k.ones_blk, sq[si], True, True, [("sq", si), "ones_blk"], [("bank", b3)])
            k.act(rr[si], k.banks[b3][:, :], AF.Sqrt, [("bank", b3)], [("rr", si)], bias=float(64 * EPS))
            k.v("dve", "reciprocal", [("rr", si)], [("rr", si)], out=rr[si], in_=rr[si])
            k.v("dve", "scalar_tensor_tensor", ["yv", ("rr", si), "gcol"], [("ob16", oi)], out=ob16[oi],
                in0=yv, scalar=gcol[:, 4 + hh:5 + hh], in1=rr[si], op0=ALU.mult, op1=ALU.mult)
            k.dma(gt[256 + hh * 128:256 + (hh + 1) * 128, tsl], ob16[oi], reads=[("ob16", oi)])

    NG = S // 512
    stage1(0)
    for g in range(NG):
        if g + 1 < NG:
            stage1(g + 1)
        stage2(g)


class GroupFin:
    def __init__(self, k, l, row0):
        self.k = k
        self.row0 = row0
        self.gout = k.sb([256], F32)
        k.dma(self.gout, k.I["g_out"][l:l + 1, row0:row0 + 256].partition_broadcast(128), writes=["gf_gout"])
        self.osb = k.sb([256], F32)
        self.sq = k.sb([256], F32)
        self.ssg = k.sb([4], F32)
        self.on = k.sb([256], BF16)
        self.gtile = [k.sb([256], BF16) for _ in range(2)]
        self.n = 0

    def run(self, src, src_keys, qb, T):
        k = self.k
        k.v("dve", "tensor_copy", list(src_keys), ["gf_osb"], out=self.osb, in_=src)
        k.v("dve", "tensor_tensor", ["gf_osb"], ["gf_sq"], out=self.sq, in0=self.osb, in1=self.osb, op=ALU.mult)
        k.v("dve", "tensor_reduce", ["gf_sq"], ["gf_ssg"], out=self.ssg,
            in_=self.sq.rearrange("p (h d) -> p h d", h=4), axis=AX.X, op=ALU.add)
        k.act(self.ssg, self.ssg, AF.Ln, ["gf_ssg"], ["gf_ssg"], scale=1.0 / 64, bias=EPS)
        k.act(self.ssg, self.ssg, AF.Exp, ["gf_ssg"], ["gf_ssg"], scale=-0.5)
        for h in range(4):
            k.v("dve", "scalar_tensor_tensor", ["gf_osb", "gf_ssg", "gf_gout"], ["gf_on"],
                out=self.on[:, h * 64:(h + 1) * 64], in0=self.osb[:, h * 64:(h + 1) * 64], scalar=self.ssg[:, h:h + 1],
                in1=self.gout[:, h * 64:(h + 1) * 64], op0=ALU.mult, op1=ALU.mult)
        b = k.next_bank()
        pb = k.banks[b][:, :].bitcast(BF16)
        for c in range(2):
            k.tr(pb[:, c * 128:(c + 1) * 128], self.on[:, c * 128:(c + 1) * 128], k.identb, ["gf_on", "identb"],
                 [("bank", b)])
        gi = self.n % 2
        self.n += 1
        k.v("dve", "tensor_copy", [("bank", b)], [("gf_gt", gi)], out=self.gtile[gi], in_=pb[:, 0:256])
        for c in range(2):
            r0 = self.row0 + c * 128
            k.dma(T["gt"][r0:r0 + 128, qb * 128:(qb + 1) * 128], self.gtile[gi][:, c * 128:(c + 1) * 128],
                  reads=[("gf_gt", gi)])


def phase_B(k, l, T):
    k.new_phase(rot_banks=[0, 1, 2, 3, 7])
    ft, tm, I = T["ft"], T["tm"], k.I
    SELBIG = 262144.0
    col = lambda ap: ap.rearrange("a (d o) -> (a d) o", o=1)
    QW = k.sb([NB, 512], BF16)
    KS2 = k.sb([S], BF16)
    KW = k.sb([S], BF16)
    EE = k.sb([S], BF16)
    VS = k.sb([NB, 65], BF16)
    VW = k.sb([NB, 65], BF16)
    k.v("pool", "memset", [], ["bVS"], VS, 1.0)
    k.v("pool", "memset", [], ["bVW"], VW, 1.0)
    for h in range(4):
        k.dma(QW[0:64, :, h * 128:(h + 1) * 128], ft[h * 64:(h + 1) * 64, :].rearrange("d (qb ql) -> d qb ql", ql=128),
              writes=["bQW"])
    k.dma_cast(QW[64:68, :, :], I["al_qa"].rearrange("r (qb c) -> r qb c", c=512), writes=["bQW"])
    k.dma(KS2[0:64, :], ft[384:448, :], writes=["bKS"])
    k.dma_cast(KS2[64:68, :], I["al_k"], writes=["bKS"])
    k.dma(KW[0:64, :], ft[448:512, :], writes=["bKW"])
    k.dma_cast(KW[64:68, :], I["al_k"], writes=["bKW"])
    k.dma_cast(EE[0:64, :], I["sel_e"], writes=["bEE"])
    k.dma(VS[:, :, 0:64], tm[:, 0:64].rearrange("(kb p) d -> p kb d", p=128), writes=["bVS"])
    k.dma(VW[:, :, 0:64], tm[:, 64:128].rearrange("(kb p) d -> p kb d", p=128), writes=["bVW"])
    G = k.sb([NB, 12], F32)
    k.dma(G, T["gates"].rearrange("(qb p) c -> p qb c", p=128), writes=["bG"])
    FB = k.sb([NB, 64], F32)
    k.dma(FB, I["sel_fb"].rearrange("p (qb j) -> p qb j", j=64), writes=["bFB"])
    cbw = k.sb([4, 504], F32)
    k.dma(cbw, I["cmp_bias"].rearrange("p (h j) -> p h j", j=504), writes=["cbw"])
    fin = GroupFin(k, l, 0)

    kca = k.sb([S], BF16)
    vca = k.sb([S], BF16)
    k.dma(kca[0:64, :], ft[256:320, :], writes=["kca"])
    k.dma(vca[0:64, :], ft[320:384, :], writes=["vca"])
    W1 = [k.sb([32, 256], BF16) for _ in range(2)]
    pe = [k.sb([34], BF16) for _ in range(2)]
    W2 = [k.sb([2, 64], BF16) for _ in range(2)]
    for kv, (w1n, pen, w2n) in enumerate((("w1_k_cmp", "pe_k_cmp", "w2_k_cmp"), ("w1_v_cmp", "pe_v_cmp", "w2_v_cmp"))):
        k.dma_cast(W1[kv][0:64, :, :], I[w1n][l].rearrange("(i d) c -> d i c", d=64), writes=[("W1", kv)])
        k.v("pool", "memset", [], [("pe", kv)], pe[kv], 0.0)
        k.dma_cast(pe[kv][0:64, 0:32], I[pen][l].rearrange("i d -> d i"), writes=[("pe", kv)], allow_slow_non_contiguous=True)
        k.dma_cast(W2[kv], I[w2n][l].rearrange("(cc p) d -> p cc d", p=128), writes=[("W2", kv)])
    gkc = k.sb([1], F32)
    k.dma(gkc[0:64, :], col(I["g_k_cmp"][l:l + 1, :]), writes=["gkc"])
    kcT = k.sb([256], BF16)
    vcmp = k.sb([2, 64], BF16)
    k.v("pool", "memset", [], ["kcT"], kcT, 0.0)
    k.v("pool", "memset", [], ["vcmp"], vcmp, 0.0)
    gl = [k.sb([2, 256], BF16) for _ in range(2)]
    hsb = k.sb([256], F32)
    uu = k.sb([256], F32)
    sgm = k.sb([256], F32)
    bvec = k.sb([1], F32)
    for kv, X in enumerate((kca, vca)):
        xk = "kca" if kv == 0 else "vca"
        for cc in range(2):
            b = k.next_bank()
            ps = k.banks[b]
            for i in range(32):
                xs = X[0:64, slice(i, min(S, i + 4080), 16)]
                assert xs.shape[1] == N_CMP
                k.mm(ps[:, 0:N_CMP], W1[kv][0:64, i, cc * 128:(cc + 1) * 128], xs, i == 0, i == 31, [xk, ("W1", kv)], [("bank", b)])
            b2 = k.next_bank()
            for i in range(32):
                k.mm(k.banks[b2][:, 0:2], W1[kv][0:64, i, cc * 128:(cc + 1) * 128], pe[kv][0:64, i:i + 2], i == 0, i == 31,
                     [("pe", kv), ("W1", kv)], [("bank", b2)])
            k.v("dve", "tensor_copy", [("bank", b2)], ["bvec"], out=bvec, in_=k.banks[b2][:, 0:1])
            k.act(hsb[:, 0:N_CMP], ps[:, 0:N_CMP], AF.Identity, [("bank", b), "bvec"], ["hsb"], bias=bvec)
            k.v("dve", "tensor_tensor", ["hsb"], ["uu"], out=uu[:, 0:N_CMP], in0=hsb[:, 0:N_CMP], in1=hsb[:, 0:N_CMP], op=ALU.mult)
            k.v("dve", "tensor_scalar", ["uu"], ["uu"], out=uu[:, 0:N_CMP], in0=uu[:, 0:N_CMP], scalar1=0.044715, scalar2=1.0,
                op0=ALU.mult, op1=ALU.add)
            k.v("dve", "tensor_tensor", ["uu", "hsb"], ["uu"], out=uu[:, 0:N_CMP], in0=uu[:, 0:N_CMP], in1=hsb[:, 0:N_CMP], op=ALU.mult)
            k.act(sgm[:, 0:N_CMP], uu[:, 0:N_CMP], AF.Sigmoid, ["uu"], ["sgm"], scale=1.5957691216057308)
            k.v("dve", "tensor_tensor", ["sgm", "hsb"], [("gl", kv)], out=gl[kv][:, cc, 0:N_CMP], in0=hsb[:, 0:N_CMP],
                in1=sgm[:, 0:N_CMP], op=ALU.mult)
        if kv == 0:
            b = k.next_bank()
            ps = k.banks[b]
            for cc in range(2):
                k.mm(ps[0:64, 0:N_CMP], W2[0][:, cc, :], gl[0][:, cc, 0:N_CMP], cc == 0, cc == 1, [("gl", 0), ("W2", 0)], [("bank", b)])
            k.v("pool", "memset", [], ["hsb"], hsb, 0.0)
            k.act(hsb[0:64, 0:N_CMP], ps[0:64, 0:N_CMP], AF.Square, [("bank", b)], ["hsb"])
            b3 = k.next_bank()
            k.mm(k.banks[b3][0:64, 0:256], k.ones_blk[0:64, 0:64], hsb[0:64, 0:256], True, True, ["hsb", "ones_blk"], [("bank", b3)])
            k.act(uu[0:64, 0:256], k.banks[b3][0:64, 0:256], AF.Ln, [("bank", b3)], ["uu"], scale=1.0 / 64, bias=EPS)
            k.act(uu[0:64, 0:256], uu[0:64, 0:256], AF.Exp, ["uu"], ["uu"], scale=-0.5)
            k.v("dve", "scalar_tensor_tensor", [("bank", b), "gkc", "uu"], ["kcT"], out=kcT[0:64, 0:N_CMP], in0=ps[0:64, 0:N_CMP],
                scalar=gkc[0:64, :], in1=uu[0:64, 0:N_CMP], op0=ALU.mult, op1=ALU.mult)
        else:
            for nt, nn in ((0, 128), (1, 127)):
                b = k.next_bank()
                for cc in range(2):
                    k.mm(k.banks[b][0:nn, 0:64], gl[1][:, cc, nt * 128:nt * 128 + nn], W2[1][:, cc, :], cc == 0, cc == 1,
                         [("gl", 1), ("W2", 1)], [("bank", b)])
                k.v("dve", "tensor_copy", [("bank", b)], ["vcmp"], out=vcmp[0:nn, nt, :], in_=k.banks[b][0:nn, 0:64])

    s2 = [k.sb([256], F32) for _ in range(4)]
    pc = [k.sb([256], F32) for _ in range(4)]
    pcb = [k.sb([256], BF16) for _ in range(4)]
    pT = k.sb([1024], BF16)
    rs = k.sb([4], F32)
    rinv = k.sb([4], F32)
    imp = k.sb([256], F32)
    iov = k.sb([256], F32)
    sc = k.sb([64], F32)
    sc2 = k.sb([64], F32)
    m8 = k.sb([16], F32)
    seln = k.sb([64], F32)
    SELT = [k.sb([512], BF16) for _ in range(2)]
    ocmp = [k.sb([256], F32) for _ in range(2)]
    pt = [k.sb([512], BF16) for _ in range(4)]
    rz = k.sb([8], F32)
    cf0 = k.sb([4], F32)
    cf = k.sb([8], F32)
    osb = k.sb([256], F32)
    asb = k.sb([260], F32)
    wsb = k.sb([260], F32)
    acc_c, acc_s, acc_w = k.banks[4], k.banks[5], k.banks[6]

    cbank = {}

    def S0(qb):
        bs = [k.next_bank(), k.next_bank()]
        cbank[qb] = bs
        for h in range(4):
            b = bs[h // 2]
            k.mm(k.banks[b][:, (h % 2) * 256:(h % 2 + 1) * 256], QW[0:64, qb, h * 128:(h + 1) * 128], kcT[0:64, :], True, True,
                 ["bQW", "kcT"], [("bank", b)], skip_group_check=True)

    def S1(qb):
        bs = cbank[qb]
        j0 = 248 - 8 * qb
        for h in range(4):
            b = bs[h // 2]
            k.v("dve", "scalar_tensor_tensor", [("bank", b), "cbw"], [("s2", h)], out=s2[h],
                in0=k.banks[b][:, (h % 2) * 256:(h % 2 + 1) * 256], scalar=0.125, in1=cbw[:, h, j0:j0 + 256], op0=ALU.mult, op1=ALU.add)

    def S2(qb):
        for h in range(4):
            k.act(pc[h], s2[h], AF.Exp, [("s2", h)], [("pc", h), ("rs", h)], accum_out=rs[:, h:h + 1])

    def S3(qb):
        rsk = [("rs", h) for h in range(4)]
        k.v("dve", "tensor_scalar_add", rsk, ["rinv"], out=rinv, in0=rs, scalar1=1.0e-30)
        k.v("dve", "reciprocal", ["rinv"], ["rinv"], out=rinv, in_=rinv)
        for h in range(4):
            k.act(pcb[h], pc[h], AF.Copy, [("pc", h)], [("pcb", h)])

    def S4(qb):
        k.v("dve", "tensor_scalar_mul", [("pc", 0), "rinv"], ["imp"], out=imp, in0=pc[0], scalar1=rinv[:, 0:1])
        for h in range(1, 4):
            k.v("dve", "scalar_tensor_tensor", [("pc", h), "rinv", "imp"], ["imp"], out=imp, in0=pc[h], scalar=rinv[:, h:h + 1],
                in1=imp, op0=ALU.mult, op1=ALU.add)
        k.v("dve", "tensor_tensor", ["imp"], ["iov"], out=iov[:, 1:256], in0=imp[:, 1:256], in1=imp[:, 0:255], op=ALU.add)
        k.v("dve", "tensor_copy", ["imp"], ["iov"], out=iov[:, 0:1], in_=imp[:, 0:1])
        k.v("dve", "tensor_reduce", ["iov"], ["sc"], out=sc, in_=iov.rearrange("p (j r) -> p j r", r=4), axis=AX.X, op=ALU.add)
        k.v("dve", "tensor_tensor", ["sc", "bFB"], ["sc"], out=sc, in0=sc, in1=FB[:, qb, :], op=ALU.add)

    def S5(qb):
        k.v("dve", "max", ["sc"], ["m8a"], out=m8[:, 0:8], in_=sc)
        k.v("dve", "match_replace", ["sc", "m8a"], ["sc2"], out=sc2, in_to_replace=m8[:, 0:8], in_values=sc, imm_value=-3.0e38)
        k.v("dve", "max", ["sc2"], ["m8b"], out=m8[:, 8:16], in_=sc2)
        k.v("dve", "tensor_scalar", ["sc", "m8b"], ["seln"], out=seln, in0=sc, scalar1=m8[:, 15:16], scalar2=1.0,
            op0=ALU.is_ge, op1=ALU.subtract)
        k.v("dve", "tensor_scalar_mul", ["seln"], ["seln"], out=seln, in0=seln, scalar1=SELBIG)

    tbank = {}

    def S6(qb):
        bS = k.next_bank()
        bT = k.next_bank()
        tbank[qb] = (bS, bT)
        k.tr(k.banks[bS][0:64, 0:128], seln, k.identf, ["seln", "identf"], [("bank", bS)])
        pbT = k.banks[bT][:, :].bitcast(BF16)
        for h in range(4):
            for nt in range(2):
                j = h * 2 + nt
                k.tr(pbT[:, j * 128:(j + 1) * 128], pcb[h][:, nt * 128:(nt + 1) * 128], k.identb, [("pcb", h), "identb"], [("bank", bT)])

    def S7(qb):
        si = qb % 2
        bS, bT = tbank[qb]
        for h in range(4):
            k.v("dve", "tensor_copy", [("bank", bS)], [("SELT", si)], out=SELT[si][0:64, h * 128:(h + 1) * 128],
                in_=k.banks[bS][0:64, 0:128])
        k.v("dve", "tensor_copy", [("bank", bT)], ["pT"], out=pT, in_=k.banks[bT][:, :].bitcast(BF16))

    def S8(qb):
        for h in range(4):
            for nt in range(2):
                j = h * 2 + nt
                k.mm(acc_c[:, h * 64:(h + 1) * 64], pT[:, j * 128:(j + 1) * 128], vcmp[:, nt, :], j == 0, j == 7, ["pT", "vcmp"],
                     [("bank", 4)], skip_group_check=True)

    def S9(qb):
        si = qb % 2
        g3 = G[:, qb, :].rearrange("p (h r) -> p h r", r=3)
        k.v("dve", "tensor_tensor", ["rinv", "bG"], ["cf0"], out=cf0, in0=rinv, in1=g3[:, :, 0], op=ALU.mult)
        for h in range(4):
            k.v("dve", "tensor_scalar_mul", [("bank", 4), "cf0"], [("ocmp", si)], out=ocmp[si][:, h * 64:(h + 1) * 64],
                in0=acc_c[:, h * 64:(h + 1) * 64], scalar1=cf0[:, h:h + 1])

    def S01(qb):
        S0(qb)
        S1(qb)

    def S67(qb):
        S6(qb)
        S7(qb)

    STAGES = (S01, S2, S3, S4, S5, S67, S8, S9)

    def make_step(qb, kb, idx, last, kind, sidx):
        i4 = sidx % 4
        si = qb % 2
        accb, bankno, Vt, vkey = (acc_s, 5, VS, "bVS") if kind == "slc" else (acc_w, 6, VW, "bVW")

        def front():
            b = k.next_bank()
            ps = k.banks[b]
            if kind == "slc":
                k.mm(ps[:, :], KS2[0:68, kb * 128:(kb + 1) * 128], QW[0:68, qb, :], True, False, ["bKS", "bQW"], [("bank", b)])
                k.mm(ps[:, :], EE[0:64, kb * 128:(kb + 1) * 128], SELT[si][0:64, :], False, True, ["bEE", ("SELT", si)], [("bank", b)])
            else:
                k.mm(ps[:, :], KW[0:68, kb * 128:(kb + 1) * 128], QW[0:68, qb, :], True, True, ["bKW", "bQW"], [("bank", b)])
            k.act(pt[i4], ps[:, :], AF.Exp, [("bank", b)], [("pt", i4)], scale=0.125)
            dl = qb - kb
            mi = None
            if dl == 0:
                mi = 0
            elif kind == "win" and dl == 4:
                mi = 2
            if mi is not None:
                k.v("dve", "tensor_tensor", [("pt", i4), "masks"], [("pt", i4)], out=pt[i4], in0=pt[i4], in1=k.masks[:, mi, :],
                    op=ALU.mult)

        def back():
            for h in range(4):
                k.mm(accb[:, h * 65:(h + 1) * 65], pt[i4][:, h * 128:(h + 1) * 128], Vt[:, kb, :], idx == 0 and h == 0, last,
                     [("pt", i4), vkey], [("bank", bankno)], skip_group_check=True)
            for _ in range(B_FILL):
                k.mm(accb[:, 260:512], k.tri_incl, k.masks[:, 0, 0:252], False, False, ["tri_incl", "masks"], [("bank", bankno)],
                     skip_group_check=True)
        return (front, back)

    def make_fin(qb):
        si = qb % 2

        def back():
            k.v("dve", "tensor_copy", [("bank", 5)], ["asb"], out=asb, in_=acc_s[:, 0:260])
            k.v("dve", "tensor_copy", [("bank", 6)], ["wsb"], out=wsb, in_=acc_w[:, 0:260])
            s3 = asb.rearrange("p (h c) -> p h c", c=65)
            w3 = wsb.rearrange("p (h c) -> p h c", c=65)
            k.v("dve", "reciprocal", ["asb"], ["rz"], out=rz[:, 0:4], in_=s3[:, :, 64])
            k.v("dve", "reciprocal", ["wsb"], ["rz"], out=rz[:, 4:8], in_=w3[:, :, 64])
            g3 = G[:, qb, :].rearrange("p (h r) -> p h r", r=3)
            k.v("dve", "tensor_tensor", ["rz", "bG"], ["cf"], out=cf[:, 0:4], in0=rz[:, 0:4], in1=g3[:, :, 1], op=ALU.mult)
            k.v("dve", "tensor_tensor", ["rz", "bG"], ["cf"], out=cf[:, 4:8], in0=rz[:, 4:8], in1=g3[:, :, 2], op=ALU.mult)
            for h in range(4):
                oh = osb[:, h * 64:(h + 1) * 64]
                k.v("dve", "scalar_tensor_tensor", ["asb", "cf", ("ocmp", si)], ["b_osb"], out=oh, in0=asb[:, h * 65:h * 65 + 64],
                    scalar=cf[:, h:h + 1], in1=ocmp[si][:, h * 64:(h + 1) * 64], op0=ALU.mult, op1=ALU.add)
                k.v("dve", "scalar_tensor_tensor", ["wsb", "cf", "b_osb"], ["b_osb"], out=oh, in0=wsb[:, h * 65:h * 65 + 64],
                    scalar=cf[:, 4 + h:5 + h], in1=oh, op0=ALU.mult, op1=ALU.add)
            fin.run(osb, ["b_osb"], qb, T)
        return (None, back)

    NQ = DBG_NQ
    items = [((lambda f=f: f(0)), None) for f in STAGES]
    sidx = 0
    for qb in range(NQ):
        steps = []
        for idx, kb in enumerate(range(qb + 1)):
            steps.append(make_step(qb, kb, idx, kb == qb, "slc", sidx))
            sidx += 1
        kbs = list(range(max(0, qb - 4), qb + 1))
        for idx, kb in enumerate(kbs):
            steps.append(make_step(qb, kb, idx, kb == qb, "win", sidx))
            sidx += 1
        nxt = [((lambda f=f, q=qb + 1: f(q)), None) for f in STAGES] if qb + 1 < NQ else []
        merged = []
        ns = len(steps)
        nst = len(STAGES)
        pos = [(j * ns) // (nst + 1) for j in range(nst)]
        for i, st in enumerate(steps):
            while nxt and pos[nst - len(nxt)] <= i:
                merged.append(nxt.pop(0))
            merged.append(st)
        merged += nxt
        items += merged
        items.append(make_fin(qb))
    run_pipeline(items, 2)


def phase_C(k, l, T):
    k.new_phase(rot_banks=[0, 1, 2])
    ft, tm = T["ft"], T["tm"]
    Q = k.sb([NB, 512], BF16)
    Kt = k.sb([4, S], BF16)
    V = k.sb([NB, 256], BF16)
    for h in range(4):
        k.dma(Q[0:64, :, h * 128:(h + 1) * 128], ft[512 + h * 64:512 + (h + 1) * 64, :].rearrange("d (qb ql) -> d qb ql", ql=128),
              writes=["cQ"])
        k.dma(Kt[0:64, h, :], ft[768 + h * 64:768 + (h + 1) * 64, :], writes=["cK"])
    k.dma(V, tm[:, 128:384].rearrange("(kb p) c -> p kb c", p=128), writes=["cV"])
    fin = GroupFin(k, l, 512)
    zs = [k.sb([512], F32) for _ in range(4)]
    ee = [k.sb([512], F32) for _ in range(2)]
    sp = [k.sb([512], F32) for _ in range(4)]
    sph = [k.sb([512], BF16) for _ in range(4)]
    spl = [k.sb([512], BF16) for _ in range(4)]
    arg = [k.sb([512], F32) for _ in range(2)]
    at = [k.sb([512], BF16) for _ in range(2)]
    mstrict = k.masks[:, 1, :]
    items = []
    step = 0
    for qb in range(NB):
        cb = 4 + (qb % 2)
        ab = 6 + (qb % 2)
        for idx, kb in enumerate(range(qb, -1, -1)):
            i3 = step % 4
            i2 = step % 2
            step += 1

            def front(qb=qb, kb=kb, i3=i3, i2=i2):
                b = k.next_bank()
                ps = k.banks[b]
                for h in range(4):
                    k.mm(ps[:, h * 128:(h + 1) * 128], Kt[0:64, h, kb * 128:(kb + 1) * 128],
                         Q[0:64, qb, h * 128:(h + 1) * 128], True, True, ["cQ", "cK"], [("bank", b)])
                k.v("dve", "tensor_scalar_mul", [("bank", b)], [("zs", i3)], out=zs[i3], in0=ps[:, :], scalar1=0.125)
                k.act(ee[i2], zs[i3], AF.Exp, [("zs", i3)], [("ee", i2)])
                k.act(sp[i3], ee[i2], AF.Ln, [("ee", i2)], [("sp", i3)], bias=1.0)

            def front_b(qb=qb, kb=kb, i3=i3, i2=i2):
                if kb == qb:
                    k.v("dve", "tensor_tensor", [("sp", i3), "masks"], [("sp", i3)], out=sp[i3], in0=sp[i3], in1=mstrict,
                        op=ALU.mult)
                k.v("dve", "tensor_copy", [("sp", i3)], [("sph", i3)], out=sph[i3], in_=sp[i3])
                k.v("pool", "tensor_tensor", [("sp", i3), ("sph", i3)], [("spl", i3)], out=spl[i3], in0=sp[i3], in1=sph[i3],
                    op=ALU.subtract)

            def back1(qb=qb, kb=kb, idx=idx, i3=i3, i2=i2, cb=cb, ab=ab):
                cps = k.banks[cb]
                k.mm(cps[:, :], k.tri_incl, sph[i3], idx == 0, False, [("sph", i3), "tri_incl"], [("bank", cb)],
                     skip_group_check=True)
                k.mm(cps[:, :], k.tri_incl, spl[i3], False, False, [("spl", i3), "tri_incl"], [("bank", cb)],
                     skip_group_check=True)
                k.v("dve", "tensor_tensor", [("zs", i3), ("bank", cb)], [("arg", i2)], out=arg[i2], in0=zs[i3], in1=cps[:, :],
                    op=ALU.subtract)
                k.act(at[i2], arg[i2], AF.Exp, [("arg", i2)], [("at", i2)])
                if kb == qb:
                    k.v("pool", "tensor_tensor", [("at", i2), "masks"], [("at", i2)], out=at[i2], in0=at[i2], in1=mstrict,
                        op=ALU.mult)

            def back2(qb=qb, kb=kb, idx=idx, i3=i3, i2=i2, cb=cb, ab=ab):
                cps = k.banks[cb]
                if kb > 0:
                    k.mm(cps[:, :], k.tri_rest, sph[i3], False, False, [("sph", i3), "tri_rest"], [("bank", cb)],
                         skip_group_check=True)
                    k.mm(cps[:, :], k.tri_rest, spl[i3], False, True, [("spl", i3), "tri_rest"], [("bank", cb)],
                         skip_group_check=True)

            def back3(qb=qb, kb=kb, idx=idx, i3=i3, i2=i2, cb=cb, ab=ab):
                aps = k.banks[ab]
                for h in range(4):
                    k.mm(aps[:, h * 64:(h + 1) * 64], at[i2][:, h * 128:(h + 1) * 128], V[:, kb, h * 64:(h + 1) * 64],
                         idx == 0 and h == 0, kb == 0, [("at", i2), "cV"], [("bank", ab)], skip_group_check=True)
                if kb == 0:
                    fin.run(k.banks[ab][:, 0:256], [("bank", ab)], qb, T)
            items.append((front, back1, back2, back3, front_b))
    n = len(items)
    junkb = k.banks[3]
    for t in range(n + 4):
        if 0 <= t - 4 < n:
            items[t - 4][2]()
        if 0 <= t - 3 < n:
            items[t - 3][1]()
        if 0 <= t - 4 < n:
            items[t - 4][3]()
        if t < n:
            items[t][0]()
        if 0 <= t - 1 < n:
            items[t - 1][4]()
        for _ in range(C_FILL):
            k.mm(junkb[:, :], k.tri_incl, k.masks[:, 0, :], True, True, ["tri_incl", "masks"], [("bank", 3)])


def phase_D(k, l, T):
    PRE = 8 * D * 2 + 8 * D_FF * 2 + 128
    k.new_phase(rot_banks=[0, 1, 2, 3, 4, 5], reserve=PRE)
    ft, tm, I = T["ft"], T["tm"], k.I
    Q = k.sb([NB, 512], BF16)
    Kt = k.sb([4, S], BF16)
    V = k.sb([NB, 4, 65], BF16)
    k.v("pool", "memset", [], ["dV"], V, 1.0)
    for h in range(4):
        k.dma(Q[0:64, :, h * 128:(h + 1) * 128], ft[1024 + h * 64:1024 + (h + 1) * 64, :].rearrange("d (qb ql) -> d qb ql", ql=128),
              writes=["dQ"])
        k.dma(Kt[0:64, h, :], ft[1280 + h * 64:1280 + (h + 1) * 64, :], writes=["dK"])
        k.dma_cast(Kt[64:68, h, :], I["al_k"], writes=["dK"])
        k.dma(V[:, :, h, 0:64], tm[:, 384 + h * 64:384 + (h + 1) * 64].rearrange("(kb p) d -> p kb d", p=128), writes=["dV"])
    k.dma_cast(Q[64:68, :, :], I["al_qd"].rearrange("r (qb c) -> r qb c", c=512), writes=["dQ"])
    save = k.off
    k.off = k.const_end
    wo = k.sb([8, D], BF16)
    wg = k.sb([8, D_FF], BF16)
    assert k.off <= k.const_end + PRE
    k.off = save
    for kc in range(8):
        k.dma_cast(wo[:, kc, :], I["w_out"][l, kc * 128:(kc + 1) * 128, :])
    for kc in range(8):
        k.dma_cast(wg[:, kc, :], I["w_gate"][l, kc * 128:(kc + 1) * 128, :])
    k.prefetched = l
    fin = GroupFin(k, l, 768)
    pt = [k.sb([512], BF16) for _ in range(4)]
    rz = k.sb([4], F32)
    osb = k.sb([256], F32)
    items = []
    step = 0
    for qb in range(DBG_NQ):
        ab = 6 + (qb % 2)
        kbs = list(range(max(0, qb - 16), qb + 1))
        for idx, kb in enumerate(kbs):
            i4 = step % 4
            step += 1
            last = idx == len(kbs) - 1

            def front(qb=qb, kb=kb, i4=i4):
                b = k.next_bank()
                ps = k.banks[b]
                for h in range(4):
                    k.mm(ps[:, h * 128:(h + 1) * 128], Kt[0:68, h, kb * 128:(kb + 1) * 128], Q[0:68, qb, h * 128:(h + 1) * 128],
                         True, True, ["dQ", "dK"], [("bank", b)])
                k.act(pt[i4], ps[:, :], AF.Exp, [("bank", b)], [("pt", i4)], scale=0.125)
                mi = 3 + k.dil_idx[qb - kb]
                k.v("dve", "tensor_tensor", [("pt", i4), "masks"], [("pt", i4)], out=pt[i4], in0=pt[i4], in1=k.masks[:, mi, :],
                    op=ALU.mult)

            def back(qb=qb, kb=kb, idx=idx, i4=i4, ab=ab, last=last):
                aps = k.banks[ab]
                for h in range(4):
                    k.mm(aps[:, h * 65:(h + 1) * 65], pt[i4][:, h * 128:(h + 1) * 128], V[:, kb, h, :],
                         idx == 0 and h == 0, last, [("pt", i4), "dV"], [("bank", ab)], skip_group_check=True)
                if last:
                    a3 = aps[:, 0:260].rearrange("p (h c) -> p h c", c=65)
                    k.v("dve", "reciprocal", [("bank", ab)], ["d_rz"], out=rz, in_=a3[:, :, 64])
                    for h in range(4):
                        k.v("dve", "tensor_scalar_mul", [("bank", ab), "d_rz"], ["d_osb"], out=osb[:, h * 64:(h + 1) * 64],
                            in0=aps[:, h * 65:h * 65 + 64], scalar1=rz[:, h:h + 1])
                    fin.run(osb, ["d_osb"], qb, T)
            items.append((front, back))
    run_pipeline(items, D_L)


def phase_E(k, l, x_src, x_dst, T):
    I = k.I
    k.new_phase()
    TT = 256
    wo = k.sb([8, D], BF16)
    wg = k.sb([8, D_FF], BF16)
    wu = k.sb([8, D_FF], BF16)
    wd = k.sb([NFC, D], BF16)
    if getattr(k, "prefetched", None) != l:
        for kc in range(8):
            k.dma_cast(wo[:, kc, :], I["w_out"][l, kc * 128:(kc + 1) * 128, :], writes=[("wo", kc)])
        for kc in range(8):
            k.dma_cast(wg[:, kc, :], I["w_gate"][l, kc * 128:(kc + 1) * 128, :], writes=[("wg", kc)])
    for kc in range(8):
        k.dma_cast(wu[:, kc, :], I["w_up"][l, kc * 128:(kc + 1) * 128, :], writes=[("wu", kc)])
    for fc in range(NFC):
        k.dma_cast(wd[:, fc, :], I["w_down"][l, fc * 128:(fc + 1) * 128, :], writes=[("wd", fc)])
    wok = [("wo", kc) for kc in range(8)]
    wgk = [("wg", kc) for kc in range(8)]
    wuk = [("wu", kc) for kc in range(8)]
    wdk = [("wd", fc) for fc in range(NFC)]
    gm32 = k.sb([D], F32)
    k.dma(gm32, I["g_ffn"][l:l + 1, :].partition_broadcast(128), writes=["gm32"])
    k.v("dve", "tensor_scalar_mul", ["gm32"], ["gm32"], out=gm32, in0=gm32, scalar1=32.0)
    gtt = [k.sb([8, TT], BF16) for _ in range(1)]
    xn = [k.sb([2, D], F32) for _ in range(1)]
    junk = k.sb([D], BF16)
    ssv = k.sb([1], F32)
    h2 = k.sb([D], BF16)
    h2T = k.sb([8, TT], BF16)
    aT = k.sb([NFC, TT], BF16)
    sg = [k.sb([TT], F32) for _ in range(2)]
    gt3 = T["gt"].rearrange("(kc p) t -> p kc t", p=128)
    nyo = 0
    for it in range(S // TT):
        gi = 0
        tok0 = it * TT
        k.dma(gtt[gi], gt3[:, :, tok0:tok0 + TT], writes=[("gtt", gi)])
        for st in range(2):
            t0 = tok0 + st * 128
            xk = ("xn", gi, st)
            k.dma(xn[gi][:, st, :], x_src[t0:t0 + 128, :], writes=[xk])
            for half in range(2):
                b = k.next_bank()
                ps = k.banks[b]
                for kc in range(8):
                    k.mm(ps[:, :], gtt[gi][:, kc, st * 128:(st + 1) * 128], wo[:, kc, half * 512:(half + 1) * 512],
                         kc == 0, kc == 7, [("gtt", gi)] + wok, [("bank", b)])
                k.v("dve", "tensor_tensor", [("bank", b), xk], [xk], out=xn[gi][:, st, half * 512:(half + 1) * 512],
                    in0=ps[:, :], in1=xn[gi][:, st, half * 512:(half + 1) * 512], op=ALU.add)
            rms_rows(k, xn[gi][:, st, :], gm32, h2, ssv, [xk], ["h2"], junk)
            b = k.next_bank()
            pb = k.banks[b][:, :].bitcast(BF16)
            for kc in range(8):
                k.tr(pb[:, kc * 128:(kc + 1) * 128], h2[:, kc * 128:(kc + 1) * 128], k.identb, ["h2", "identb"], [("bank", b)])
            k.v("dve", "tensor_copy", [("bank", b)], ["h2T"], out=h2T[:, :, st * 128:(st + 1) * 128],
                in_=pb.rearrange("p (a b) -> p a b", a=8))
        for fc in range(NFC):
            b = k.next_bank()
            ps = k.banks[b]
            for kc in range(8):
                k.mm(ps[:, 0:TT], wg[:, kc, fc * 128:(fc + 1) * 128], h2T[:, kc, :], kc == 0, kc == 7, ["h2T"] + wgk,
                     [("bank", b)])
            for kc in range(8):
                k.mm(ps[:, TT:2 * TT], wu[:, kc, fc * 128:(fc + 1) * 128], h2T[:, kc, :], kc == 0, kc == 7, ["h2T"] + wuk,
                     [("bank", b)], skip_group_check=True)
            si = fc % 2
            k.act(sg[si], ps[:, 0:TT], AF.Silu, [("bank", b)], [("sg", si)])
            k.v("dve", "tensor_tensor", [("sg", si), ("bank", b)], ["aT"], out=aT[:, fc, :], in0=sg[si], in1=ps[:, TT:2 * TT],
                op=ALU.mult)
        for st in range(2):
            t0 = tok0 + st * 128
            xk = ("xn", gi, st)
            yi = nyo % 2
            nyo += 1
            for half in range(2):
                b = k.next_bank()
                ps = k.banks[b]
                for fc in range(NFC):
                    k.mm(ps[:, :], aT[:, fc, st * 128:(st + 1) * 128], wd[:, fc, half * 512:(half + 1) * 512],
                         fc == 0, fc == NFC - 1, ["aT"] + wdk, [("bank", b)])
                k.v("dve", "tensor_tensor", [("bank", b), xk], [xk], out=xn[gi][:, st, half * 512:(half + 1) * 512],
                    in0=ps[:, :], in1=xn[gi][:, st, half * 512:(half + 1) * 512], op=ALU.add)
            k.dma(x_dst[t0:t0 + 128, :], xn[gi][:, st, :], reads=[xk])


def build(shapes, cinfo, phases=("A",), layers=(0,), dbg=()):
    nc = bass.Bass("TRN2", target_bir_lowering=False)
    with ExitStack() as st:
        k = KB(nc, st, shapes, dbg=set(dbg))
        T = {
            "ft": k.dram("ft", [1536, S], BF16),
            "tm": k.dram("tm", [S, 640], BF16),
            "gates": k.dram("gates", [S, 12], F32),
            "gt": k.dram("gt", [1024, S], BF16),
            "xres": k.dram("xres", [S, D], F32),
        }
        k.load_consts(cinfo)
        for l in layers:
            x_src = k.I["x"] if l == 0 else T["xres"]
            x_dst = k.y if l == DEPTH - 1 else T["xres"]
            if "A" in phases:
                phase_A(k, l, x_src, T)
            if "B" in phases:
                phase_B(k, l, T)
            if "C" in phases:
                phase_C(k, l, T)
            if "D" in phases:
                phase_D(k, l, T)
            if "E" in phases:
                phase_E(k, l, x_src, x_dst, T)
        k.P.emit(st)
    return nc


def prep_inputs(inputs):
    c = host_consts()
    cinfo = {"nm": int(c["masks"].shape[0]), "dil_idx": c["_dil_idx"]}
    shared = {n: np.ascontiguousarray(c[n], dtype=np.float32) for n in CONST_NAMES}
    for n in PARAM_NAMES:
        shared[n] = np.ascontiguousarray(np.asarray(inputs[n], dtype=np.float32))
    shapes = {n: a.shape for n, a in shared.items()}
    shapes["x"] = (S, D)
    return shared, shapes, cinfo


def kernel(**inputs):
    shared, shapes, cinfo = prep_inputs(inputs)
    x = np.asarray(inputs["x"], dtype=np.float32)
    nc = build(shapes, cinfo, phases=("A", "B", "C", "D", "E"), layers=(0, 1))
    in_maps = []
    for b in range(8):
        m = dict(shared)
        m["x"] = np.ascontiguousarray(x[b])
        in_maps.append(m)
    res = run_bass_kernel_spmd(nc, in_maps, core_ids=list(range(8)))
    return np.stack([np.asarray(r["y"], dtype=np.float32) for r in res.results], 0)
```

```python
import numpy as np
from contextlib import ExitStack
import concourse.bass as bass
import concourse.mybir as mybir
from concourse.bass_utils import run_bass_kernel_spmd

F32 = mybir.dt.float32
BF16 = mybir.dt.bfloat16
U8 = mybir.dt.uint8
AF = mybir.ActivationFunctionType
ALU = mybir.AluOpType
AX = mybir.AxisListType

S = 4096
D = 1024
NB = S // 128
HD = 64
DEPTH = 2
IN_COLS = 2956
D_FF = 2816
NFC = D_FF // 128
N_CMP = 255
EPS = 1e-6
BIG = 30000.0
SLOPES = [2.0 ** (-(i + 1)) for i in range(8)]
SLOPES_A = SLOPES[0::2]
SLOPES_D = SLOPES[1::2]
DIL_CFG = ((128, 1), (512, 4), (2048, 16))

EPOCH = 8000
DBG_NQ = NB
D_L = 1
C_FILL = 3
NDMA_SLOTS = 32


class Op:
    __slots__ = ("eng", "fn", "deps", "is_dma", "slot", "slot_val", "slot_prev", "need_inc")


class Prog:
    ENGS = ("pe", "act", "dve", "pool", "sp")

    def __init__(self, nc):
        self.nc = nc
        self.ops = []
        self.last_w = {}
        self.readers = {}
        self.streams = {e: [] for e in self.ENGS}
        self.dma_cnt = {}
        self.pending_fence = {}
        self.dma_since_fence = []

    def op(self, eng, fn, reads=(), writes=(), dma=False):
        o = Op()
        o.eng, o.fn, o.is_dma = eng, fn, dma
        deps = set()
        for k in reads:
            w = self.last_w.get(k)
            if w is not None:
                deps.add(w)
        for k in writes:
            w = self.last_w.get(k)
            if w is not None:
                deps.add(w)
            for r in self.readers.get(k, ()):
                deps.add(r)
        f = self.pending_fence.pop(eng, None)
        if f:
            deps |= f
        gi = len(self.ops)
        o.deps = deps
        o.need_inc = False
        o.slot = None
        self.ops.append(o)
        self.streams[eng].append(gi)
        for k in reads:
            self.readers.setdefault(k, []).append(gi)
        for k in writes:
            self.last_w[k] = gi
            self.readers[k] = []
        if dma:
            half = NDMA_SLOTS // 2
            c = self.dma_cnt.get(eng, 0)
            self.dma_cnt[eng] = c + 1
            o.slot = (c % half) + (half if eng == "pool" else 0)
            self.dma_since_fence.append(gi)
        return gi

    def fence(self):
        f = set(self.dma_since_fence)
        for e in self.ENGS:
            if self.streams[e]:
                f.add(self.streams[e][-1])
        for e in self.ENGS:
            prev = self.pending_fence.get(e)
            self.pending_fence[e] = set(f) | (prev or set())
        self.dma_since_fence = []
        self.last_w = {}
        self.readers = {}

    def emit(self, stack):
        nc = self.nc
        ops = self.ops
        for o in ops:
            for d in o.deps:
                od = ops[d]
                if od.is_dma:
                    continue
                if od.eng == "pe" and o.eng == "pe" and not o.is_dma:
                    continue
                od.need_inc = True
        cnt = {e: 0 for e in self.ENGS}
        val = {}
        for gi, o in enumerate(ops):
            if o.is_dma:
                continue
            if o.need_inc:
                cnt[o.eng] += 1
                val[gi] = cnt[o.eng]
        nep = {e: (cnt[e] + EPOCH - 1) // EPOCH + 1 for e in self.ENGS}
        sems = {e: [stack.enter_context(nc.semaphore(f"s_{e}_{i}")) for i in range(nep[e])] for e in self.ENGS}
        dsems = [stack.enter_context(nc.semaphore(f"s_dma_{i}")) for i in range(NDMA_SLOTS)]
        slot_tot = [0] * NDMA_SLOTS
        slot_last = [None] * NDMA_SLOTS
        for gi, o in enumerate(ops):
            if o.is_dma:
                o.slot_prev = slot_last[o.slot]
                slot_tot[o.slot] += 16
                o.slot_val = slot_tot[o.slot]
                slot_last[o.slot] = gi

        def sem_of(gi):
            o = ops[gi]
            if o.is_dma:
                return dsems[o.slot], o.slot_val, ("d", o.slot)
            v = val[gi] - 1
            return sems[o.eng][v // EPOCH], v % EPOCH + 1, (o.eng, v // EPOCH)

        def run_engine(ename, eng):
            waited = {}
            for gi in self.streams[ename]:
                o = ops[gi]
                deps = set(o.deps)
                if o.is_dma and o.slot_prev is not None:
                    deps.add(o.slot_prev)
                need = {}
                for d in deps:
                    od = ops[d]
                    if (not od.is_dma) and od.eng == "pe" and ename == "pe" and not o.is_dma:
                        continue
                    s, v, key = sem_of(d)
                    if key not in need or need[key][1] < v:
                        need[key] = (s, v)
                for key, (s, v) in need.items():
                    if waited.get(key, 0) >= v:
                        continue
                    eng.wait_ge(s, v)
                    waited[key] = v
                ins = o.fn(eng)
                if o.is_dma:
                    ins.then_inc(dsems[o.slot], 16)
                elif o.need_inc:
                    s, v, key = sem_of(gi)
                    ins.then_inc(s, 1)
            if ename in ("sp", "pool"):
                lastv = {}
                for gi in self.streams[ename]:
                    o = ops[gi]
                    if o.is_dma:
                        lastv[o.slot] = max(lastv.get(o.slot, 0), o.slot_val)
                for sl, v in lastv.items():
                    eng.wait_ge(dsems[sl], v)

        with nc.Block() as block:
            @block.tensor
            def _(e):
                run_engine("pe", e)

            @block.scalar
            def _(e):
                run_engine("act", e)

            @block.vector
            def _(e):
                run_engine("dve", e)

            @block.gpsimd
            def _(e):
                run_engine("pool", e)

            @block.sync
            def _(e):
                run_engine("sp", e)


def host_consts():
    c = {}
    i = np.arange(128)
    c["ident"] = np.eye(128, dtype=np.float32)
    c["tri_incl"] = (i[:, None] >= i[None, :]).astype(np.float32)
    c["tri_rest"] = 1.0 - c["tri_incl"]
    ob = np.zeros((128, 128), np.float32)
    ob[:64, :64] = 1.0
    ob[64:, 64:] = 1.0
    c["ones_blk"] = ob
    kl = i[:, None]
    ql = i[None, :]
    rep4 = lambda m: np.tile(m.astype(np.float32), (1, 4))
    masks = [rep4(kl <= ql), rep4(kl < ql), rep4(ql < kl)]
    dil_tiles = []
    dil_idx = []
    for dl in range(17):
        d = 128 * dl + ql - kl
        m = np.zeros((128, 128), np.float32)
        for window, dil in DIL_CFG:
            m += ((d >= 0) & (d <= window) & (d % dil == 0)).astype(np.float32)
        found = None
        for ti, t in enumerate(dil_tiles):
            if np.array_equal(t, m):
                found = ti
        if found is None:
            dil_tiles.append(m)
            found = len(dil_tiles) - 1
        dil_idx.append(found)
    c["_dil_idx"] = dil_idx
    c["_n_dil"] = len(dil_tiles)
    masks += [rep4(t) for t in dil_tiles]
    c["masks"] = np.ascontiguousarray(np.stack(masks, 0))
    t = np.arange(S)
    a_t, b_t = t // 64, t % 64
    alk = np.stack([-np.ones(S), -np.ones(S), a_t, b_t], 0).astype(np.float32)
    c["al_k"] = alk

    def qrows(slopes):
        r = np.zeros((4, 4, S), np.float32)
        for h, s in enumerate(slopes):
            r[0, h] = 8 * s * 64 * a_t
            r[1, h] = 8 * s * b_t
            r[2, h] = 8 * s * 64
            r[3, h] = 8 * s
        return r
    qa = qrows(SLOPES_A)
    c["al_qa"] = np.ascontiguousarray(qa.reshape(4, 4, NB, 128).transpose(0, 2, 1, 3).reshape(4, NB * 512))
    c["al_qd"] = np.ascontiguousarray(qrows(SLOPES_D).reshape(4, 4, NB, 128).transpose(0, 2, 1, 3).reshape(4, NB * 512))
    jj = np.arange(504)
    dd = (i[:, None] - 16 * (jj[None, :] - 248) - 31).astype(np.float64)
    cb = np.zeros((128, 4, 504), np.float32)
    for h, sl in enumerate(SLOPES_A):
        cb[:, h, :] = np.where(dd >= 0, -sl * dd, 1.0e4 * dd)
    c["cmp_bias"] = cb.reshape(128, 4 * 504)
    fb = np.zeros((128, NB, 64), np.float32)
    j = np.arange(64)
    for qb in range(NB):
        tt = qb * 128 + i
        cur = tt // 64
        forced = (j[None] == 0) | (j[None] == cur[:, None]) | (j[None] == cur[:, None] - 1)
        valid = j[None] * 64 <= tt[:, None]
        fb[:, qb, :] = np.where(forced, 1e6, np.where(valid, 0.0, -1e6))
    c["sel_fb"] = fb.reshape(128, NB * 64)
    e = np.zeros((64, S), np.float32)
    e[t // 64, t] = 1.0
    c["sel_e"] = e
    return c


CONST_NAMES = ["ident", "tri_incl", "tri_rest", "ones_blk", "masks", "al_k", "al_qa", "al_qd", "cmp_bias", "sel_fb", "sel_e"]
PARAM_NAMES = ["g_mix", "w_in", "b_gate", "g_q_nsa", "g_k_cmp", "g_k_slc", "g_k_win", "pe_k_cmp", "pe_v_cmp",
               "w1_k_cmp", "w2_k_cmp", "w1_v_cmp", "w2_v_cmp", "conv_w", "g_q_dil", "g_k_dil", "g_out", "w_out",
               "g_ffn", "w_gate", "w_up", "w_down"]


class KB:
    def __init__(self, nc, stack, shapes, dbg=None):
        self.nc = nc
        self.P = Prog(nc)
        self.dbg = dbg or {}
        self.I = {}
        for name, shp in shapes.items():
            self.I[name] = nc.dram_tensor(name, list(shp), F32, kind="ExternalInput").ap()
        self.y = nc.dram_tensor("y", [S, D], F32, kind="ExternalOutput").ap()
        ARENA = 204 * 1024
        self.arena_t = stack.enter_context(nc.sbuf_tensor("arena", [128, ARENA], U8))
        self.arena_size = ARENA
        self.off = 0
        self.banks = [stack.enter_context(nc.psum_tensor(f"bank{i}", [128, 512], F32)) for i in range(8)]
        self.rot = 0
        self.rot_banks = list(range(8))
        self.uid = 0

    def dram(self, name, shape, dt):
        kind = "ExternalOutput" if name in self.dbg else "Internal"
        return self.nc.dram_tensor(name, list(shape), dt, kind=kind).ap()

    def sb(self, free_shape, dt, parts=128):
        n = int(np.prod(free_shape))
        esz = {F32: 4, BF16: 2, U8: 1}[dt]
        nbytes = n * esz
        self.off = (self.off + 63) // 64 * 64
        assert self.off + nbytes <= self.arena_size, f"arena overflow {self.off + nbytes}"
        ap = self.arena_t[0:parts, self.off:self.off + nbytes]
        if dt != U8:
            ap = ap.bitcast(dt)
        self.off += nbytes
        if len(free_shape) == 2:
            ap = ap.rearrange("p (a b) -> p a b", a=free_shape[0], b=free_shape[1])
        elif len(free_shape) == 3:
            ap = ap.rearrange("p (a b c) -> p a b c", a=free_shape[0], b=free_shape[1], c=free_shape[2])
        return ap

    def key(self, base):
        self.uid += 1
        return f"{base}#{self.uid}"

    def next_bank(self):
        b = self.rot_banks[self.rot % len(self.rot_banks)]
        self.rot += 1
        return b

    def new_phase(self, rot_banks=None, reserve=0):
        self.P.fence()
        self.off = self.const_end + reserve
        self.rot_banks = rot_banks or list(range(8))
        self.rot = 0

    def dma(self, out, in_, reads=(), writes=(), eng="sp", **kw):
        return self.P.op(eng, lambda e: e.dma_start(out=out, in_=in_, **kw), reads, writes, dma=True)

    def dma_cast(self, out, in_, reads=(), writes=(), **kw):
        return self.P.op("pool", lambda e: e.dma_start(out=out, in_=in_, max_dma_last_dim=4096, **kw), reads, writes, dma=True)

    def mm(self, out, lhsT, rhs, start, stop, reads, writes, **kw):
        return self.P.op("pe", lambda e: e.matmul(out, lhsT=lhsT, rhs=rhs, start=start, stop=stop, **kw), reads, writes)

    def tr(self, out, in_, ident, reads, writes):
        return self.P.op("pe", lambda e: e.transpose(out=out, in_=in_, identity=ident), reads, writes)

    def act(self, out, in_, func, reads, writes, **kw):
        return self.P.op("act", lambda e: e.activation(out=out, in_=in_, func=func, **kw), reads, writes)

    def v(self, eng, method, reads, writes, *a, **kw):
        return self.P.op(eng, lambda e: getattr(e, method)(*a, **kw), reads, writes)

    def load_consts(self, cinfo):
        I = self.I
        self.off = 0
        self.identb = self.sb([128], BF16)
        self.identf = self.sb([128], F32)
        self.tri_incl = self.sb([128], BF16)
        self.tri_rest = self.sb([128], BF16)
        self.ones_blk = self.sb([128], F32)
        nm = cinfo["nm"]
        self.masks = self.sb([nm, 512], BF16)
        self.dma_cast(self.identb, I["ident"], writes=["identb"])
        self.dma(self.identf, I["ident"], writes=["identf"])
        self.dma_cast(self.tri_incl, I["tri_incl"], writes=["tri_incl"])
        self.dma_cast(self.tri_rest, I["tri_rest"], writes=["tri_rest"])
        self.dma(self.ones_blk, I["ones_blk"], writes=["ones_blk"])
        for m in range(nm):
            self.dma_cast(self.masks[:, m, :], I["masks"][m], writes=["masks"])
        self.const_end = self.off
        self.dil_idx = cinfo["dil_idx"]


def run_pipeline(items, L):
    n = len(items)
    for i in range(n + L):
        if i < n and items[i][0] is not None:
            items[i][0]()
        j = i - L
        if j >= 0 and items[j][1] is not None:
            items[j][1]()


def rms_rows(k, x_t, gm32, hn_out, ss, rk, wk, junk):
    k.act(junk, x_t, AF.Square, rk, ["junk", "ss"], accum_out=ss)
    k.act(ss, ss, AF.Sqrt, ["ss"], ["ss"], bias=float(D * EPS))
    k.v("dve", "reciprocal", ["ss"], ["ss"], out=ss, in_=ss)
    k.v("dve", "scalar_tensor_tensor", list(rk) + ["ss", "gm32"], wk, out=hn_out, in0=x_t, scalar=ss, in1=gm32,
        op0=ALU.mult, op1=ALU.mult)


def phase_A(k, l, x_src, T):
    I, P = k.I, k.P
    k.new_phase()
    w = k.sb([8, IN_COLS], BF16)
    for kc in range(8):
        k.dma_cast(w[:, kc, :], I["w_in"][l, kc * 128:(kc + 1) * 128, :], writes=[("w", kc)])
    wkeys = [("w", kc) for kc in range(8)]
    gm32 = k.sb([D], F32)
    k.dma(gm32, I["g_mix"][l:l + 1, :].partition_broadcast(128), writes=["gm32"])
    k.v("dve", "tensor_scalar_mul", ["gm32"], ["gm32"], out=gm32, in0=gm32, scalar1=32.0)
    gcol = k.sb([6], F32)
    col = lambda ap: ap.rearrange("a (d o) -> (a d) o", o=1)
    for (c, nm) in ((0, "g_q_nsa"), (2, "g_q_dil"), (3, "g_k_dil")):
        for hf in range(2):
            k.dma(gcol[hf * 64:(hf + 1) * 64, c:c + 1], col(I[nm][l:l + 1, :]), writes=["gcol"])
    k.dma(gcol[0:64, 1:2], col(I["g_k_slc"][l:l + 1, :]), writes=["gcol"])
    k.dma(gcol[64:128, 1:2], col(I["g_k_win"][l:l + 1, :]), writes=["gcol"])
    for hh in range(2):
        k.dma(gcol[:, 4 + hh:5 + hh], col(I["g_out"][l:l + 1, 256 + hh * 128:256 + (hh + 1) * 128]), writes=["gcol"])
    k.v("dve", "tensor_scalar_mul", ["gcol"], ["gcol"], out=gcol, in0=gcol, scalar1=8.0)
    cw = k.sb([2, 3], F32)
    for hh in range(2):
        k.dma(cw[:, hh, :], I["conv_w"][l, :, hh * 128:(hh + 1) * 128].rearrange("k p -> p k"), writes=["cw"],
              allow_slow_non_contiguous=True)
    bg = k.sb([12], F32)
    k.dma(bg, I["b_gate"][l:l + 1, :].partition_broadcast(128), writes=["bg"])

    xt = [k.sb([D], F32) for _ in range(2)]
    junk = k.sb([D], F32)
    ssv = k.sb([1], F32)
    hn = [k.sb([D], BF16) for _ in range(2)]
    hnT = [k.sb([8, 512], BF16) for _ in range(2)]
    sq = [k.sb([512], F32) for _ in range(2)]
    rr = [k.sb([512], F32) for _ in range(2)]
    ob16 = [k.sb([512], BF16) for _ in range(3)]
    tmo = [k.sb([640], BF16) for _ in range(2)]
    gts = [k.sb([12], F32) for _ in range(2)]
    cu = [k.sb([514], F32) for _ in range(2)]
    csb = k.sb([512], F32)
    yv = k.sb([512], F32)
    for hh in range(2):
        k.v("pool", "memset", [], [("cu", hh)], cu[hh], 0.0)

    def wcols(c0, n=128):
        return lambda kc: w[:, kc, c0:c0 + n]

    def wcols_ks_kw(kc):
        return w[:, kc, 384:640].rearrange("p (a b) -> p a b", b=128)[:, :, 0:64]
    fblocks = [
        (wcols(0), "norm", 0, 0), (wcols(128), "norm", 0, 128), (wcols(256), "raw", None, 256),
        ("kskw", "norm", 1, 384),
        (wcols(1420), "raw", None, 512), (wcols(1548), "raw", None, 640),
        (wcols(1676), "raw", None, 768), (wcols(1804), "raw", None, 896),
        (wcols(2188), "norm", 2, 1024), (wcols(2316), "norm", 2, 1152),
        (wcols(2444), "norm", 3, 1280), (wcols(2572), "norm", 3, 1408),
    ]
    ft, tm, gates, gt = T["ft"], T["tm"], T["gates"], T["gt"]
    nob = [0]

    def stage1(g):
        hT = hnT[g % 2]
        hk = ("hnT", g % 2)
        for tt in range(4):
            tok0 = g * 512 + tt * 128
            ti = (g * 4 + tt) % 2
            k.dma(xt[ti], x_src[tok0:tok0 + 128, :], writes=[("xt", ti)])
            rms_rows(k, xt[ti], gm32, hn[ti], ssv, [("xt", ti)], [("hn", ti)], junk)
            b = k.next_bank()
            pb = k.banks[b][:, :].bitcast(BF16)
            for kc in range(8):
                k.tr(pb[:, kc * 128:(kc + 1) * 128], hn[ti][:, kc * 128:(kc + 1) * 128], k.identb,
                     [("hn", ti), "identb"], [("bank", b)])
            k.v("dve", "tensor_copy", [("bank", b)], [hk], out=hT[:, :, tt * 128:(tt + 1) * 128],
                in_=pb.rearrange("p (a b) -> p a b", a=8))
            to = tmo[ti]
            for (c0, n, dst0) in ((448, 204, None), (1932, 256, 128), (2700, 256, 384)):
                b2 = k.next_bank()
                ps = k.banks[b2]
                for kc in range(8):
                    k.mm(ps[:, 0:n], hT[:, kc, tt * 128:(tt + 1) * 128], w[:, kc, c0:c0 + n], kc == 0, kc == 7,
                         [hk] + wkeys, [("bank", b2)])
                if dst0 is None:
                    k.v("dve", "tensor_copy", [("bank", b2)], [("tmo", ti)], out=to[:, 0:64], in_=ps[:, 0:64])
                    k.v("dve", "tensor_copy", [("bank", b2)], [("tmo", ti)], out=to[:, 64:128], in_=ps[:, 128:192])
                    k.v("dve", "tensor_tensor", [("bank", b2), "bg"], [("gts", ti)], out=gts[ti], in0=ps[:, 192:204],
                        in1=bg, op=ALU.add)
                    k.act(gts[ti], gts[ti], AF.Sigmoid, [("gts", ti)], [("gts", ti)])
                    k.dma(gates[tok0:tok0 + 128, :], gts[ti], reads=[("gts", ti)])
                else:
                    k.act(to[:, dst0:dst0 + 256], ps[:, 0:256], AF.Copy, [("bank", b2)], [("tmo", ti)])
            k.dma(tm[tok0:tok0 + 128, :], to, reads=[("tmo", ti)])

    def stage2(g):
        hT = hnT[g % 2]
        hk = ("hnT", g % 2)
        tsl = slice(g * 512, (g + 1) * 512)
        for (wc, kind, gc, r0) in fblocks:
            b = k.next_bank()
            ps = k.banks[b]
            if wc == "kskw":
                for (p0, c0) in ((0, 384), (64, 512)):
                    for kc in range(8):
                        k.mm(ps[p0:p0 + 64, :], w[:, kc, c0:c0 + 64], hT[:, kc, :], kc == 0, kc == 7, [hk] + wkeys,
                             [("bank", b)])
            else:
                for kc in range(8):
                    k.mm(ps[:, :], wc(kc), hT[:, kc, :], kc == 0, kc == 7, [hk] + wkeys, [("bank", b)])
            oi = nob[0] % 3
            nob[0] += 1
            if kind == "raw":
                k.act(ob16[oi], ps[:, :], AF.Copy, [("bank", b)], [("ob16", oi)])
            else:
                si = nob[0] % 2
                k.act(sq[si], ps[:, :], AF.Square, [("bank", b)], [("sq", si)])
                b3 = k.next_bank()
                k.mm(k.banks[b3][:, :], k.ones_blk, sq[si], True, True, [("sq", si), "ones_blk"], [("bank", b3)])
                k.act(rr[si], k.banks[b3][:, :], AF.Sqrt, [("bank", b3)], [("rr", si)], bias=float(64 * EPS))
                k.v("dve", "reciprocal", [("rr", si)], [("rr", si)], out=rr[si], in_=rr[si])
                k.v("dve", "scalar_tensor_tensor", [("bank", b), ("rr", si), "gcol"], [("ob16", oi)], out=ob16[oi],
                    in0=ps[:, :], scalar=gcol[:, gc:gc + 1], in1=rr[si], op0=ALU.mult, op1=ALU.mult)
            k.dma(ft[r0:r0 + 128, tsl], ob16[oi], reads=[("ob16", oi)])
        for hh in range(2):
            bb, bc, bu = k.next_bank(), k.next_bank(), k.next_bank()
            for (bx, c0) in ((bb, 652), (bc, 908), (bu, 1164)):
                for kc in range(8):
                    k.mm(k.banks[bx][:, :], w[:, kc, c0 + hh * 128:c0 + (hh + 1) * 128], hT[:, kc, :], kc == 0, kc == 7,
                         [hk] + wkeys, [("bank", bx)])
            k.act(csb, k.banks[bc][:, :], AF.Copy, [("bank", bc)], ["csb"])
            k.v("dve", "tensor_tensor", ["csb", ("bank", bu)], [("cu", hh)], out=cu[hh][:, 2:514], in0=csb,
                in1=k.banks[bu][:, :], op=ALU.mult)
            k.v("dve", "tensor_scalar_mul", [("cu", hh), "cw"], ["yv"], out=yv, in0=cu[hh][:, 0:512],
                scalar1=cw[:, hh, 0:1])
            k.v("dve", "scalar_tensor_tensor", [("cu", hh), "cw", "yv"], ["yv"], out=yv, in0=cu[hh][:, 1:513],
                scalar=cw[:, hh, 1:2], in1=yv, op0=ALU.mult, op1=ALU.add)
            k.v("dve", "scalar_tensor_tensor", [("cu", hh), "cw", "yv"], ["yv"], out=yv, in0=cu[hh][:, 2:514],
                scalar=cw[:, hh, 2:3], in1=yv, op0=ALU.mult, op1=ALU.add)
            k.v("dve", "tensor_tensor", ["yv", ("bank", bb)], ["yv"], out=yv, in0=yv, in1=k.banks[bb][:, :], op=ALU.mult)
            k.v("pool", "tensor_copy", [("cu", hh), "yv"], [("cu", hh)], out=cu[hh][:, 0:2], in_=cu[hh][:, 512:514])
            si = nob[0] % 2
            oi = nob[0] % 3
            nob[0] += 1
            k.act(sq[si], yv, AF.Square, ["yv"], [("sq", si)])
            b3 = k.next_bank()
            k.mm(k.banks[b3][:, :], k.ones_blk, sq[si], True, True, [("sq", si), "ones_blk"], [("bank", b3)])
            k.act(rr[si], k.banks[b3][:, :], AF.Sqrt, [("bank", b3)], [("rr", si)], bias=float(64 * EPS))
            k.v("dve", "reciprocal", [("rr", si)], [("rr", si)], out=rr[si], in_=rr[si])
            k.v("dve", "scalar_tensor_tensor", ["yv", ("rr", si), "gcol"], [("ob16", oi)], out=ob16[oi],
                in0=yv, scalar=gcol[:, 4 + hh:5 + hh], in1=rr[si], op0=ALU.mult, op1=ALU.mult)
            k.dma(gt[256 + hh * 128:256 + (hh + 1) * 128, tsl], ob16[oi], reads=[("ob16", oi)])

    NG = S // 512
    stage1(0)
    for g in range(NG):
        if g + 1 < NG:
            stage1(g + 1)
        stage2(g)


class GroupFin:
    def __init__(self, k, l, row0):
        self.k = k
        self.row0 = row0
        self.gout = k.sb([256], F32)
        k.dma(self.gout, k.I["g_out"][l:l + 1, row0:row0 + 256].partition_broadcast(128), writes=["gf_gout"])
        self.osb = k.sb([256], F32)
        self.sq = k.sb([256], F32)
        self.ssg = k.sb([4], F32)
        self.on = k.sb([256], BF16)
        self.gtile = [k.sb([256], BF16) for _ in range(2)]
        self.n = 0

    def run(self, src, src_keys, qb, T):
        k = self.k
        k.v("dve", "tensor_copy", list(src_keys), ["gf_osb"], out=self.osb, in_=src)
        k.v("dve", "tensor_tensor", ["gf_osb"], ["gf_sq"], out=self.sq, in0=self.osb, in1=self.osb, op=ALU.mult)
        k.v("dve", "tensor_reduce", ["gf_sq"], ["gf_ssg"], out=self.ssg,
            in_=self.sq.rearrange("p (h d) -> p h d", h=4), axis=AX.X, op=ALU.add)
        k.act(self.ssg, self.ssg, AF.Ln, ["gf_ssg"], ["gf_ssg"], scale=1.0 / 64, bias=EPS)
        k.act(self.ssg, self.ssg, AF.Exp, ["gf_ssg"], ["gf_ssg"], scale=-0.5)
        for h in range(4):
            k.v("dve", "scalar_tensor_tensor", ["gf_osb", "gf_ssg", "gf_gout"], ["gf_on"],
                out=self.on[:, h * 64:(h + 1) * 64], in0=self.osb[:, h * 64:(h + 1) * 64], scalar=self.ssg[:, h:h + 1],
                in1=self.gout[:, h * 64:(h + 1) * 64], op0=ALU.mult, op1=ALU.mult)
        b = k.next_bank()
        pb = k.banks[b][:, :].bitcast(BF16)
        for c in range(2):
            k.tr(pb[:, c * 128:(c + 1) * 128], self.on[:, c * 128:(c + 1) * 128], k.identb, ["gf_on", "identb"],
                 [("bank", b)])
        gi = self.n % 2
        self.n += 1
        k.v("dve", "tensor_copy", [("bank", b)], [("gf_gt", gi)], out=self.gtile[gi], in_=pb[:, 0:256])
        for c in range(2):
            r0 = self.row0 + c * 128
            k.dma(T["gt"][r0:r0 + 128, qb * 128:(qb + 1) * 128], self.gtile[gi][:, c * 128:(c + 1) * 128],
                  reads=[("gf_gt", gi)])


def phase_B(k, l, T):
    k.new_phase(rot_banks=[0, 1, 2, 3, 7])
    ft, tm, I = T["ft"], T["tm"], k.I
    SELBIG = 262144.0
    col = lambda ap: ap.rearrange("a (d o) -> (a d) o", o=1)
    QW = k.sb([NB, 512], BF16)
    KS2 = k.sb([S], BF16)
    KW = k.sb([S], BF16)
    EE = k.sb([S], BF16)
    VS = k.sb([NB, 65], BF16)
    VW = k.sb([NB, 65], BF16)
    k.v("pool", "memset", [], ["bVS"], VS, 1.0)
    k.v("pool", "memset", [], ["bVW"], VW, 1.0)
    for h in range(4):
        k.dma(QW[0:64, :, h * 128:(h + 1) * 128], ft[h * 64:(h + 1) * 64, :].rearrange("d (qb ql) -> d qb ql", ql=128),
              writes=["bQW"])
    k.dma_cast(QW[64:68, :, :], I["al_qa"].rearrange("r (qb c) -> r qb c", c=512), writes=["bQW"])
    k.dma(KS2[0:64, :], ft[384:448, :], writes=["bKS"])
    k.dma_cast(KS2[64:68, :], I["al_k"], writes=["bKS"])
    k.dma(KW[0:64, :], ft[448:512, :], writes=["bKW"])
    k.dma_cast(KW[64:68, :], I["al_k"], writes=["bKW"])
    k.dma_cast(EE[0:64, :], I["sel_e"], writes=["bEE"])
    k.dma(VS[:, :, 0:64], tm[:, 0:64].rearrange("(kb p) d -> p kb d", p=128), writes=["bVS"])
    k.dma(VW[:, :, 0:64], tm[:, 64:128].rearrange("(kb p) d -> p kb d", p=128), writes=["bVW"])
    G = k.sb([NB, 12], F32)
    k.dma(G, T["gates"].rearrange("(qb p) c -> p qb c", p=128), writes=["bG"])
    FB = k.sb([NB, 64], F32)
    k.dma(FB, I["sel_fb"].rearrange("p (qb j) -> p qb j", j=64), writes=["bFB"])
    cbw = k.sb([4, 504], F32)
    k.dma(cbw, I["cmp_bias"].rearrange("p (h j) -> p h j", j=504), writes=["cbw"])
    fin = GroupFin(k, l, 0)

    kca = k.sb([S], BF16)
    vca = k.sb([S], BF16)
    k.dma(kca[0:64, :], ft[256:320, :], writes=["kca"])
    k.dma(vca[0:64, :], ft[320:384, :], writes=["vca"])
    W1 = [k.sb([32, 256], BF16) for _ in range(2)]
    pe = [k.sb([34], BF16) for _ in range(2)]
    W2 = [k.sb([2, 64], BF16) for _ in range(2)]
    for kv, (w1n, pen, w2n) in enumerate((("w1_k_cmp", "pe_k_cmp", "w2_k_cmp"), ("w1_v_cmp", "pe_v_cmp", "w2_v_cmp"))):
        k.dma_cast(W1[kv][0:64, :, :], I[w1n][l].rearrange("(i d) c -> d i c", d=64), writes=[("W1", kv)])
        k.v("pool", "memset", [], [("pe", kv)], pe[kv], 0.0)
        k.dma_cast(pe[kv][0:64, 0:32], I[pen][l].rearrange("i d -> d i"), writes=[("pe", kv)], allow_slow_non_contiguous=True)
        k.dma_cast(W2[kv], I[w2n][l].rearrange("(cc p) d -> p cc d", p=128), writes=[("W2", kv)])
    gkc = k.sb([1], F32)
    k.dma(gkc[0:64, :], col(I["g_k_cmp"][l:l + 1, :]), writes=["gkc"])
    kcT = k.sb([256], BF16)
    vcmp = k.sb([2, 64], BF16)
    k.v("pool", "memset", [], ["kcT"], kcT, 0.0)
    k.v("pool", "memset", [], ["vcmp"], vcmp, 0.0)
    gl = [k.sb([2, 256], BF16) for _ in range(2)]
    hsb = k.sb([256], F32)
    uu = k.sb([256], F32)
    sgm = k.sb([256], F32)
    bvec = k.sb([1], F32)
    for kv, X in enumerate((kca, vca)):
        xk = "kca" if kv == 0 else "vca"
        for cc in range(2):
            b = k.next_bank()
            ps = k.banks[b]
            for i in range(32):
                xs = X[0:64, slice(i, min(S, i + 4080), 16)]
                assert xs.shape[1] == N_CMP
                k.mm(ps[:, 0:N_CMP], W1[kv][0:64, i, cc * 128:(cc + 1) * 128], xs, i == 0, i == 31, [xk, ("W1", kv)], [("bank", b)])
            b2 = k.next_bank()
            for i in range(32):
                k.mm(k.banks[b2][:, 0:2], W1[kv][0:64, i, cc * 128:(cc + 1) * 128], pe[kv][0:64, i:i + 2], i == 0, i == 31,
                     [("pe", kv), ("W1", kv)], [("bank", b2)])
            k.v("dve", "tensor_copy", [("bank", b2)], ["bvec"], out=bvec, in_=k.banks[b2][:, 0:1])
            k.act(hsb[:, 0:N_CMP], ps[:, 0:N_CMP], AF.Identity, [("bank", b), "bvec"], ["hsb"], bias=bvec)
            k.v("dve", "tensor_tensor", ["hsb"], ["uu"], out=uu[:, 0:N_CMP], in0=hsb[:, 0:N_CMP], in1=hsb[:, 0:N_CMP], op=ALU.mult)
            k.v("dve", "tensor_scalar", ["uu"], ["uu"], out=uu[:, 0:N_CMP], in0=uu[:, 0:N_CMP], scalar1=0.044715, scalar2=1.0,
                op0=ALU.mult, op1=ALU.add)
            k.v("dve", "tensor_tensor", ["uu", "hsb"], ["uu"], out=uu[:, 0:N_CMP], in0=uu[:, 0:N_CMP], in1=hsb[:, 0:N_CMP], op=ALU.mult)
            k.act(sgm[:, 0:N_CMP], uu[:, 0:N_CMP], AF.Sigmoid, ["uu"], ["sgm"], scale=1.5957691216057308)
            k.v("dve", "tensor_tensor", ["sgm", "hsb"], [("gl", kv)], out=gl[kv][:, cc, 0:N_CMP], in0=hsb[:, 0:N_CMP],
                in1=sgm[:, 0:N_CMP], op=ALU.mult)
        if kv == 0:
            b = k.next_bank()
            ps = k.banks[b]
            for cc in range(2):
                k.mm(ps[0:64, 0:N_CMP], W2[0][:, cc, :], gl[0][:, cc, 0:N_CMP], cc == 0, cc == 1, [("gl", 0), ("W2", 0)], [("bank", b)])
            k.v("pool", "memset", [], ["hsb"], hsb, 0.0)
            k.act(hsb[0:64, 0:N_CMP], ps[0:64, 0:N_CMP], AF.Square, [("bank", b)], ["hsb"])
            b3 = k.next_bank()
            k.mm(k.banks[b3][0:64, 0:256], k.ones_blk[0:64, 0:64], hsb[0:64, 0:256], True, True, ["hsb", "ones_blk"], [("bank", b3)])
            k.act(uu[0:64, 0:256], k.banks[b3][0:64, 0:256], AF.Ln, [("bank", b3)], ["uu"], scale=1.0 / 64, bias=EPS)
            k.act(uu[0:64, 0:256], uu[0:64, 0:256], AF.Exp, ["uu"], ["uu"], scale=-0.5)
            k.v("dve", "scalar_tensor_tensor", [("bank", b), "gkc", "uu"], ["kcT"], out=kcT[0:64, 0:N_CMP], in0=ps[0:64, 0:N_CMP],
                scalar=gkc[0:64, :], in1=uu[0:64, 0:N_CMP], op0=ALU.mult, op1=ALU.mult)
        else:
            for nt, nn in ((0, 128), (1, 127)):
                b = k.next_bank()
                for cc in range(2):
                    k.mm(k.banks[b][0:nn, 0:64], gl[1][:, cc, nt * 128:nt * 128 + nn], W2[1][:, cc, :], cc == 0, cc == 1,
                         [("gl", 1), ("W2", 1)], [("bank", b)])
                k.v("dve", "tensor_copy", [("bank", b)], ["vcmp"], out=vcmp[0:nn, nt, :], in_=k.banks[b][0:nn, 0:64])

    s2 = [k.sb([256], F32) for _ in range(4)]
    pc = [k.sb([256], F32) for _ in range(4)]
    pcb = [k.sb([256], BF16) for _ in range(4)]
    pT = k.sb([1024], BF16)
    rs = k.sb([4], F32)
    rinv = k.sb([4], F32)
    imp = k.sb([256], F32)
    iov = k.sb([256], F32)
    sc = k.sb([64], F32)
    sc2 = k.sb([64], F32)
    m8 = k.sb([16], F32)
    seln = k.sb([64], F32)
    SELT = [k.sb([512], BF16) for _ in range(2)]
    ocmp = [k.sb([256], F32) for _ in range(2)]
    pt = [k.sb([512], BF16) for _ in range(4)]
    rz = k.sb([8], F32)
    cf0 = k.sb([4], F32)
    cf = k.sb([8], F32)
    osb = k.sb([256], F32)
    asb = k.sb([260], F32)
    wsb = k.sb([260], F32)
    acc_c, acc_s, acc_w = k.banks[4], k.banks[5], k.banks[6]

    cbank = {}

    def S0(qb):
        bs = [k.next_bank(), k.next_bank()]
        cbank[qb] = bs
        for h in range(4):
            b = bs[h // 2]
            k.mm(k.banks[b][:, (h % 2) * 256:(h % 2 + 1) * 256], QW[0:64, qb, h * 128:(h + 1) * 128], kcT[0:64, :], True, True,
                 ["bQW", "kcT"], [("bank", b)], skip_group_check=True)

    def S1(qb):
        bs = cbank[qb]
        j0 = 248 - 8 * qb
        for h in range(4):
            b = bs[h // 2]
            k.v("dve", "scalar_tensor_tensor", [("bank", b), "cbw"], [("s2", h)], out=s2[h],
                in0=k.banks[b][:, (h % 2) * 256:(h % 2 + 1) * 256], scalar=0.125, in1=cbw[:, h, j0:j0 + 256], op0=ALU.mult, op1=ALU.add)

    def S2(qb):
        for h in range(4):
            k.act(pc[h], s2[h], AF.Exp, [("s2", h)], [("pc", h), ("rs", h)], accum_out=rs[:, h:h + 1])

    def S3(qb):
        rsk = [("rs", h) for h in range(4)]
        k.v("dve", "tensor_scalar_add", rsk, ["rinv"], out=rinv, in0=rs, scalar1=1.0e-30)
        k.v("dve", "reciprocal", ["rinv"], ["rinv"], out=rinv, in_=rinv)
        for h in range(4):
            k.act(pcb[h], pc[h], AF.Copy, [("pc", h)], [("pcb", h)])

    def S4(qb):
        k.v("dve", "tensor_scalar_mul", [("pc", 0), "rinv"], ["imp"], out=imp, in0=pc[0], scalar1=rinv[:, 0:1])
        for h in range(1, 4):
            k.v("dve", "scalar_tensor_tensor", [("pc", h), "rinv", "imp"], ["imp"], out=imp, in0=pc[h], scalar=rinv[:, h:h + 1],
                in1=imp, op0=ALU.mult, op1=ALU.add)
        k.v("dve", "tensor_tensor", ["imp"], ["iov"], out=iov[:, 1:256], in0=imp[:, 1:256], in1=imp[:, 0:255], op=ALU.add)
        k.v("dve", "tensor_copy", ["imp"], ["iov"], out=iov[:, 0:1], in_=imp[:, 0:1])
        k.v("dve", "tensor_reduce", ["iov"], ["sc"], out=sc, in_=iov.rearrange("p (j r) -> p j r", r=4), axis=AX.X, op=ALU.add)
        k.v("dve", "tensor_tensor", ["sc", "bFB"], ["sc"], out=sc, in0=sc, in1=FB[:, qb, :], op=ALU.add)

    def S5(qb):
        k.v("dve", "max", ["sc"], ["m8a"], out=m8[:, 0:8], in_=sc)
        k.v("dve", "match_replace", ["sc", "m8a"], ["sc2"], out=sc2, in_to_replace=m8[:, 0:8], in_values=sc, imm_value=-3.0e38)
        k.v("dve", "max", ["sc2"], ["m8b"], out=m8[:, 8:16], in_=sc2)
        k.v("dve", "tensor_scalar", ["sc", "m8b"], ["seln"], out=seln, in0=sc, scalar1=m8[:, 15:16], scalar2=1.0,
            op0=ALU.is_ge, op1=ALU.subtract)
        k.v("dve", "tensor_scalar_mul", ["seln"], ["seln"], out=seln, in0=seln, scalar1=SELBIG)

    tbank = {}

    def S6(qb):
        bS = k.next_bank()
        bT = k.next_bank()
        tbank[qb] = (bS, bT)
        k.tr(k.banks[bS][0:64, 0:128], seln, k.identf, ["seln", "identf"], [("bank", bS)])
        pbT = k.banks[bT][:, :].bitcast(BF16)
        for h in range(4):
            for nt in range(2):
                j = h * 2 + nt
                k.tr(pbT[:, j * 128:(j + 1) * 128], pcb[h][:, nt * 128:(nt + 1) * 128], k.identb, [("pcb", h), "identb"], [("bank", bT)])

    def S7(qb):
        si = qb % 2
        bS, bT = tbank[qb]
        for h in range(4):
            k.v("dve", "tensor_copy", [("bank", bS)], [("SELT", si)], out=SELT[si][0:64, h * 128:(h + 1) * 128],
                in_=k.banks[bS][0:64, 0:128])
        k.v("dve", "tensor_copy", [("bank", bT)], ["pT"], out=pT, in_=k.banks[bT][:, :].bitcast(BF16))

    def S8(qb):
        for h in range(4):
            for nt in range(2):
                j = h * 2 + nt
                k.mm(acc_c[:, h * 64:(h + 1) * 64], pT[:, j * 128:(j + 1) * 128], vcmp[:, nt, :], j == 0, j == 7, ["pT", "vcmp"],
                     [("bank", 4)], skip_group_check=True)

    def S9(qb):
        si = qb % 2
        g3 = G[:, qb, :].rearrange("p (h r) -> p h r", r=3)
        k.v("dve", "tensor_tensor", ["rinv", "bG"], ["cf0"], out=cf0, in0=rinv, in1=g3[:, :, 0], op=ALU.mult)
        for h in range(4):
            k.v("dve", "tensor_scalar_mul", [("bank", 4), "cf0"], [("ocmp", si)], out=ocmp[si][:, h * 64:(h + 1) * 64],
                in0=acc_c[:, h * 64:(h + 1) * 64], scalar1=cf0[:, h:h + 1])

    def S01(qb):
        S0(qb)
        S1(qb)

    def S67(qb):
        S6(qb)
        S7(qb)

    STAGES = (S01, S2, S3, S4, S5, S67, S8, S9)

    def make_step(qb, kb, idx, last, kind, sidx):
        i4 = sidx % 4
        si = qb % 2
        accb, bankno, Vt, vkey = (acc_s, 5, VS, "bVS") if kind == "slc" else (acc_w, 6, VW, "bVW")

        def front():
            b = k.next_bank()
            ps = k.banks[b]
            if kind == "slc":
                k.mm(ps[:, :], KS2[0:68, kb * 128:(kb + 1) * 128], QW[0:68, qb, :], True, False, ["bKS", "bQW"], [("bank", b)])
                k.mm(ps[:, :], EE[0:64, kb * 128:(kb + 1) * 128], SELT[si][0:64, :], False, True, ["bEE", ("SELT", si)], [("bank", b)])
            else:
                k.mm(ps[:, :], KW[0:68, kb * 128:(kb + 1) * 128], QW[0:68, qb, :], True, True, ["bKW", "bQW"], [("bank", b)])
            k.act(pt[i4], ps[:, :], AF.Exp, [("bank", b)], [("pt", i4)], scale=0.125)
            dl = qb - kb
            mi = None
            if dl == 0:
                mi = 0
            elif kind == "win" and dl == 4:
                mi = 2
            if mi is not None:
                k.v("dve", "tensor_tensor", [("pt", i4), "masks"], [("pt", i4)], out=pt[i4], in0=pt[i4], in1=k.masks[:, mi, :],
                    op=ALU.mult)

        def back():
            for h in range(4):
                k.mm(accb[:, h * 65:(h + 1) * 65], pt[i4][:, h * 128:(h + 1) * 128], Vt[:, kb, :], idx == 0 and h == 0, last,
                     [("pt", i4), vkey], [("bank", bankno)], skip_group_check=True)
        return (front, back)

    def make_fin(qb):
        si = qb % 2

        def back():
            k.v("dve", "tensor_copy", [("bank", 5)], ["asb"], out=asb, in_=acc_s[:, 0:260])
            k.v("dve", "tensor_copy", [("bank", 6)], ["wsb"], out=wsb, in_=acc_w[:, 0:260])
            s3 = asb.rearrange("p (h c) -> p h c", c=65)
            w3 = wsb.rearrange("p (h c) -> p h c", c=65)
            k.v("dve", "reciprocal", ["asb"], ["rz"], out=rz[:, 0:4], in_=s3[:, :, 64])
            k.v("dve", "reciprocal", ["wsb"], ["rz"], out=rz[:, 4:8], in_=w3[:, :, 64])
            g3 = G[:, qb, :].rearrange("p (h r) -> p h r", r=3)
            k.v("dve", "tensor_tensor", ["rz", "bG"], ["cf"], out=cf[:, 0:4], in0=rz[:, 0:4], in1=g3[:, :, 1], op=ALU.mult)
            k.v("dve", "tensor_tensor", ["rz", "bG"], ["cf"], out=cf[:, 4:8], in0=rz[:, 4:8], in1=g3[:, :, 2], op=ALU.mult)
            for h in range(4):
                oh = osb[:, h * 64:(h + 1) * 64]
                k.v("dve", "scalar_tensor_tensor", ["asb", "cf", ("ocmp", si)], ["b_osb"], out=oh, in0=asb[:, h * 65:h * 65 + 64],
                    scalar=cf[:, h:h + 1], in1=ocmp[si][:, h * 64:(h + 1) * 64], op0=ALU.mult, op1=ALU.add)
                k.v("dve", "scalar_tensor_tensor", ["wsb", "cf", "b_osb"], ["b_osb"], out=oh, in0=wsb[:, h * 65:h * 65 + 64],
                    scalar=cf[:, 4 + h:5 + h], in1=oh, op0=ALU.mult, op1=ALU.add)
            fin.run(osb, ["b_osb"], qb, T)
        return (None, back)

    NQ = DBG_NQ
    items = [((lambda f=f: f(0)), None) for f in STAGES]
    sidx = 0
    for qb in range(NQ):
        steps = []
        for idx, kb in enumerate(range(qb + 1)):
            steps.append(make_step(qb, kb, idx, kb == qb, "slc", sidx))
            sidx += 1
        kbs = list(range(max(0, qb - 4), qb + 1))
        for idx, kb in enumerate(kbs):
            steps.append(make_step(qb, kb, idx, kb == qb, "win", sidx))
            sidx += 1
        nxt = [((lambda f=f, q=qb + 1: f(q)), None) for f in STAGES] if qb + 1 < NQ else []
        merged = []
        ns = len(steps)
        nst = len(STAGES)
        pos = [(j * ns) // (nst + 1) for j in range(nst)]
        for i, st in enumerate(steps):
            while nxt and pos[nst - len(nxt)] <= i:
                merged.append(nxt.pop(0))
            merged.append(st)
        merged += nxt
        items += merged
        items.append(make_fin(qb))
    run_pipeline(items, 2)


def phase_C(k, l, T):
    k.new_phase(rot_banks=[0, 1, 2])
    ft, tm = T["ft"], T["tm"]
    Q = k.sb([NB, 512], BF16)
    Kt = k.sb([4, S], BF16)
    V = k.sb([NB, 256], BF16)
    for h in range(4):
        k.dma(Q[0:64, :, h * 128:(h + 1) * 128], ft[512 + h * 64:512 + (h + 1) * 64, :].rearrange("d (qb ql) -> d qb ql", ql=128),
              writes=["cQ"])
        k.dma(Kt[0:64, h, :], ft[768 + h * 64:768 + (h + 1) * 64, :], writes=["cK"])
    k.dma(V, tm[:, 128:384].rearrange("(kb p) c -> p kb c", p=128), writes=["cV"])
    fin = GroupFin(k, l, 512)
    zs = [k.sb([512], F32) for _ in range(4)]
    ee = [k.sb([512], F32) for _ in range(2)]
    sp = [k.sb([512], F32) for _ in range(4)]
    sph = [k.sb([512], BF16) for _ in range(4)]
    spl = [k.sb([512], BF16) for _ in range(4)]
    arg = [k.sb([512], F32) for _ in range(2)]
    at = [k.sb([512], BF16) for _ in range(2)]
    mstrict = k.masks[:, 1, :]
    items = []
    step = 0
    for qb in range(NB):
        cb = 4 + (qb % 2)
        ab = 6 + (qb % 2)
        for idx, kb in enumerate(range(qb, -1, -1)):
            i3 = step % 4
            i2 = step % 2
            step += 1

            def front(qb=qb, kb=kb, i3=i3, i2=i2):
                b = k.next_bank()
                ps = k.banks[b]
                for h in range(4):
                    k.mm(ps[:, h * 128:(h + 1) * 128], Kt[0:64, h, kb * 128:(kb + 1) * 128],
                         Q[0:64, qb, h * 128:(h + 1) * 128], True, True, ["cQ", "cK"], [("bank", b)])
                k.v("dve", "tensor_scalar_mul", [("bank", b)], [("zs", i3)], out=zs[i3], in0=ps[:, :], scalar1=0.125)
                k.act(ee[i2], zs[i3], AF.Exp, [("zs", i3)], [("ee", i2)])
                k.act(sp[i3], ee[i2], AF.Ln, [("ee", i2)], [("sp", i3)], bias=1.0)

            def front_b(qb=qb, kb=kb, i3=i3, i2=i2):
                if kb == qb:
                    k.v("dve", "tensor_tensor", [("sp", i3), "masks"], [("sp", i3)], out=sp[i3], in0=sp[i3], in1=mstrict,
                        op=ALU.mult)
                k.v("dve", "tensor_copy", [("sp", i3)], [("sph", i3)], out=sph[i3], in_=sp[i3])
                k.v("pool", "tensor_tensor", [("sp", i3), ("sph", i3)], [("spl", i3)], out=spl[i3], in0=sp[i3], in1=sph[i3],
                    op=ALU.subtract)

            def back1(qb=qb, kb=kb, idx=idx, i3=i3, i2=i2, cb=cb, ab=ab):
                cps = k.banks[cb]
                k.mm(cps[:, :], k.tri_incl, sph[i3], idx == 0, False, [("sph", i3), "tri_incl"], [("bank", cb)],
                     skip_group_check=True)
                k.mm(cps[:, :], k.tri_incl, spl[i3], False, False, [("spl", i3), "tri_incl"], [("bank", cb)],
                     skip_group_check=True)
                k.v("dve", "tensor_tensor", [("zs", i3), ("bank", cb)], [("arg", i2)], out=arg[i2], in0=zs[i3], in1=cps[:, :],
                    op=ALU.subtract)
                k.act(at[i2], arg[i2], AF.Exp, [("arg", i2)], [("at", i2)])
                if kb == qb:
                    k.v("pool", "tensor_tensor", [("at", i2), "masks"], [("at", i2)], out=at[i2], in0=at[i2], in1=mstrict,
                        op=ALU.mult)

            def back2(qb=qb, kb=kb, idx=idx, i3=i3, i2=i2, cb=cb, ab=ab):
                cps = k.banks[cb]
                if kb > 0:
                    k.mm(cps[:, :], k.tri_rest, sph[i3], False, False, [("sph", i3), "tri_rest"], [("bank", cb)],
                         skip_group_check=True)
                    k.mm(cps[:, :], k.tri_rest, spl[i3], False, True, [("spl", i3), "tri_rest"], [("bank", cb)],
                         skip_group_check=True)

            def back3(qb=qb, kb=kb, idx=idx, i3=i3, i2=i2, cb=cb, ab=ab):
                aps = k.banks[ab]
                for h in range(4):
                    k.mm(aps[:, h * 64:(h + 1) * 64], at[i2][:, h * 128:(h + 1) * 128], V[:, kb, h * 64:(h + 1) * 64],
                         idx == 0 and h == 0, kb == 0, [("at", i2), "cV"], [("bank", ab)], skip_group_check=True)
                if kb == 0:
                    fin.run(k.banks[ab][:, 0:256], [("bank", ab)], qb, T)
            items.append((front, back1, back2, back3, front_b))
    n = len(items)
    junkb = k.banks[3]
    for t in range(n + 4):
        if 0 <= t - 4 < n:
            items[t - 4][2]()
        if 0 <= t - 3 < n:
            items[t - 3][1]()
        if 0 <= t - 4 < n:
            items[t - 4][3]()
        if t < n:
            items[t][0]()
        if 0 <= t - 1 < n:
            items[t - 1][4]()
        for _ in range(C_FILL):
            k.mm(junkb[:, :], k.tri_incl, k.masks[:, 0, :], True, True, ["tri_incl", "masks"], [("bank", 3)])


def phase_D(k, l, T):
    PRE = 8 * D * 2 + 8 * D_FF * 2 + 128
    k.new_phase(rot_banks=[0, 1, 2, 3, 4, 5], reserve=PRE)
    ft, tm, I = T["ft"], T["tm"], k.I
    Q = k.sb([NB, 512], BF16)
    Kt = k.sb([4, S], BF16)
    V = k.sb([NB, 4, 65], BF16)
    k.v("pool", "memset", [], ["dV"], V, 1.0)
    for h in range(4):
        k.dma(Q[0:64, :, h * 128:(h + 1) * 128], ft[1024 + h * 64:1024 + (h + 1) * 64, :].rearrange("d (qb ql) -> d qb ql", ql=128),
              writes=["dQ"])
        k.dma(Kt[0:64, h, :], ft[1280 + h * 64:1280 + (h + 1) * 64, :], writes=["dK"])
        k.dma_cast(Kt[64:68, h, :], I["al_k"], writes=["dK"])
        k.dma(V[:, :, h, 0:64], tm[:, 384 + h * 64:384 + (h + 1) * 64].rearrange("(kb p) d -> p kb d", p=128), writes=["dV"])
    k.dma_cast(Q[64:68, :, :], I["al_qd"].rearrange("r (qb c) -> r qb c", c=512), writes=["dQ"])
    save = k.off
    k.off = k.const_end
    wo = k.sb([8, D], BF16)
    wg = k.sb([8, D_FF], BF16)
    assert k.off <= k.const_end + PRE
    k.off = save
    for kc in range(8):
        k.dma_cast(wo[:, kc, :], I["w_out"][l, kc * 128:(kc + 1) * 128, :])
    for kc in range(8):
        k.dma_cast(wg[:, kc, :], I["w_gate"][l, kc * 128:(kc + 1) * 128, :])
    k.prefetched = l
    fin = GroupFin(k, l, 768)
    pt = [k.sb([512], BF16) for _ in range(4)]
    rz = k.sb([4], F32)
    osb = k.sb([256], F32)
    items = []
    step = 0
    for qb in range(DBG_NQ):
        ab = 6 + (qb % 2)
        kbs = list(range(max(0, qb - 16), qb + 1))
        for idx, kb in enumerate(kbs):
            i4 = step % 4
            step += 1
            last = idx == len(kbs) - 1

            def front(qb=qb, kb=kb, i4=i4):
                b = k.next_bank()
                ps = k.banks[b]
                for h in range(4):
                    k.mm(ps[:, h * 128:(h + 1) * 128], Kt[0:68, h, kb * 128:(kb + 1) * 128], Q[0:68, qb, h * 128:(h + 1) * 128],
                         True, True, ["dQ", "dK"], [("bank", b)])
                k.act(pt[i4], ps[:, :], AF.Exp, [("bank", b)], [("pt", i4)], scale=0.125)
                mi = 3 + k.dil_idx[qb - kb]
                k.v("dve", "tensor_tensor", [("pt", i4), "masks"], [("pt", i4)], out=pt[i4], in0=pt[i4], in1=k.masks[:, mi, :],
                    op=ALU.mult)

            def back(qb=qb, kb=kb, idx=idx, i4=i4, ab=ab, last=last):
                aps = k.banks[ab]
                for h in range(4):
                    k.mm(aps[:, h * 65:(h + 1) * 65], pt[i4][:, h * 128:(h + 1) * 128], V[:, kb, h, :],
                         idx == 0 and h == 0, last, [("pt", i4), "dV"], [("bank", ab)], skip_group_check=True)
                if last:
                    a3 = aps[:, 0:260].rearrange("p (h c) -> p h c", c=65)
                    k.v("dve", "reciprocal", [("bank", ab)], ["d_rz"], out=rz, in_=a3[:, :, 64])
                    for h in range(4):
                        k.v("dve", "tensor_scalar_mul", [("bank", ab), "d_rz"], ["d_osb"], out=osb[:, h * 64:(h + 1) * 64],
                            in0=aps[:, h * 65:h * 65 + 64], scalar1=rz[:, h:h + 1])
                    fin.run(osb, ["d_osb"], qb, T)
            items.append((front, back))
    run_pipeline(items, D_L)


def phase_E(k, l, x_src, x_dst, T):
    I = k.I
    k.new_phase()
    TT = 256
    wo = k.sb([8, D], BF16)
    wg = k.sb([8, D_FF], BF16)
    wu = k.sb([8, D_FF], BF16)
    wd = k.sb([NFC, D], BF16)
    if getattr(k, "prefetched", None) != l:
        for kc in range(8):
            k.dma_cast(wo[:, kc, :], I["w_out"][l, kc * 128:(kc + 1) * 128, :], writes=[("wo", kc)])
        for kc in range(8):
            k.dma_cast(wg[:, kc, :], I["w_gate"][l, kc * 128:(kc + 1) * 128, :], writes=[("wg", kc)])
    for kc in range(8):
        k.dma_cast(wu[:, kc, :], I["w_up"][l, kc * 128:(kc + 1) * 128, :], writes=[("wu", kc)])
    for fc in range(NFC):
        k.dma_cast(wd[:, fc, :], I["w_down"][l, fc * 128:(fc + 1) * 128, :], writes=[("wd", fc)])
    wok = [("wo", kc) for kc in range(8)]
    wgk = [("wg", kc) for kc in range(8)]
    wuk = [("wu", kc) for kc in range(8)]
    wdk = [("wd", fc) for fc in range(NFC)]
    gm32 = k.sb([D], F32)
    k.dma(gm32, I["g_ffn"][l:l + 1, :].partition_broadcast(128), writes=["gm32"])
    k.v("dve", "tensor_scalar_mul", ["gm32"], ["gm32"], out=gm32, in0=gm32, scalar1=32.0)
    gtt = [k.sb([8, TT], BF16) for _ in range(1)]
    xn = [k.sb([2, D], F32) for _ in range(1)]
    junk = k.sb([D], BF16)
    ssv = k.sb([1], F32)
    h2 = k.sb([D], BF16)
    h2T = k.sb([8, TT], BF16)
    aT = k.sb([NFC, TT], BF16)
    sg = [k.sb([TT], F32) for _ in range(2)]
    gt3 = T["gt"].rearrange("(kc p) t -> p kc t", p=128)
    nyo = 0
    for it in range(S // TT):
        gi = 0
        tok0 = it * TT
        k.dma(gtt[gi], gt3[:, :, tok0:tok0 + TT], writes=[("gtt", gi)])
        for st in range(2):
            t0 = tok0 + st * 128
            xk = ("xn", gi, st)
            k.dma(xn[gi][:, st, :], x_src[t0:t0 + 128, :], writes=[xk])
            for half in range(2):
                b = k.next_bank()
                ps = k.banks[b]
                for kc in range(8):
                    k.mm(ps[:, :], gtt[gi][:, kc, st * 128:(st + 1) * 128], wo[:, kc, half * 512:(half + 1) * 512],
                         kc == 0, kc == 7, [("gtt", gi)] + wok, [("bank", b)])
                k.v("dve", "tensor_tensor", [("bank", b), xk], [xk], out=xn[gi][:, st, half * 512:(half + 1) * 512],
                    in0=ps[:, :], in1=xn[gi][:, st, half * 512:(half + 1) * 512], op=ALU.add)
            rms_rows(k, xn[gi][:, st, :], gm32, h2, ssv, [xk], ["h2"], junk)
            b = k.next_bank()
            pb = k.banks[b][:, :].bitcast(BF16)
            for kc in range(8):
                k.tr(pb[:, kc * 128:(kc + 1) * 128], h2[:, kc * 128:(kc + 1) * 128], k.identb, ["h2", "identb"], [("bank", b)])
            k.v("dve", "tensor_copy", [("bank", b)], ["h2T"], out=h2T[:, :, st * 128:(st + 1) * 128],
                in_=pb.rearrange("p (a b) -> p a b", a=8))
        for fc in range(NFC):
            b = k.next_bank()
            ps = k.banks[b]
            for kc in range(8):
                k.mm(ps[:, 0:TT], wg[:, kc, fc * 128:(fc + 1) * 128], h2T[:, kc, :], kc == 0, kc == 7, ["h2T"] + wgk,
                     [("bank", b)])
            for kc in range(8):
                k.mm(ps[:, TT:2 * TT], wu[:, kc, fc * 128:(fc + 1) * 128], h2T[:, kc, :], kc == 0, kc == 7, ["h2T"] + wuk,
                     [("bank", b)], skip_group_check=True)
            si = fc % 2
            k.act(sg[si], ps[:, 0:TT], AF.Silu, [("bank", b)], [("sg", si)])
            k.v("dve", "tensor_tensor", [("sg", si), ("bank", b)], ["aT"], out=aT[:, fc, :], in0=sg[si], in1=ps[:, TT:2 * TT],
                op=ALU.mult)
        for st in range(2):
            t0 = tok0 + st * 128
            xk = ("xn", gi, st)
            yi = nyo % 2
            nyo += 1
            for half in range(2):
                b = k.next_bank()
                ps = k.banks[b]
                for fc in range(NFC):
                    k.mm(ps[:, :], aT[:, fc, st * 128:(st + 1) * 128], wd[:, fc, half * 512:(half + 1) * 512],
                         fc == 0, fc == NFC - 1, ["aT"] + wdk, [("bank", b)])
                k.v("dve", "tensor_tensor", [("bank", b), xk], [xk], out=xn[gi][:, st, half * 512:(half + 1) * 512],
                    in0=ps[:, :], in1=xn[gi][:, st, half * 512:(half + 1) * 512], op=ALU.add)
            k.dma(x_dst[t0:t0 + 128, :], xn[gi][:, st, :], reads=[xk])


def build(shapes, cinfo, phases=("A",), layers=(0,), dbg=()):
    nc = bass.Bass("TRN2", target_bir_lowering=False)
    with ExitStack() as st:
        k = KB(nc, st, shapes, dbg=set(dbg))
        T = {
            "ft": k.dram("ft", [1536, S], BF16),
            "tm": k.dram("tm", [S, 640], BF16),
            "gates": k.dram("gates", [S, 12], F32),
            "gt": k.dram("gt", [1024, S], BF16),
            "xres": k.dram("xres", [S, D], F32),
        }
        k.load_consts(cinfo)
        for l in layers:
            x_src = k.I["x"] if l == 0 else T["xres"]
            x_dst = k.y if l == DEPTH - 1 else T["xres"]
            if "A" in phases:
                phase_A(k, l, x_src, T)
            if "B" in phases:
                phase_B(k, l, T)
            if "C" in phases:
                phase_C(k, l, T)
            if "D" in phases:
                phase_D(k, l, T)
            if "E" in phases:
                phase_E(k, l, x_src, x_dst, T)
        k.P.emit(st)
    return nc


def prep_inputs(inputs):
    c = host_consts()
    cinfo = {"nm": int(c["masks"].shape[0]), "dil_idx": c["_dil_idx"]}
    shared = {n: np.ascontiguousarray(c[n], dtype=np.float32) for n in CONST_NAMES}
    for n in PARAM_NAMES:
        shared[n] = np.ascontiguousarray(np.asarray(inputs[n], dtype=np.float32))
    shapes = {n: a.shape for n, a in shared.items()}
    shapes["x"] = (S, D)
    return shared, shapes, cinfo


def kernel(**inputs):
    shared, shapes, cinfo = prep_inputs(inputs)
    x = np.asarray(inputs["x"], dtype=np.float32)
    nc = build(shapes, cinfo, phases=("A", "B", "C", "D", "E"), layers=(0, 1))
    in_maps = []
    for b in range(8):
        m = dict(shared)
        m["x"] = np.ascontiguousarray(x[b])
        in_maps.append(m)
    res = run_bass_kernel_spmd(nc, in_maps, core_ids=list(range(8)))
    return np.stack([np.asarray(r["y"], dtype=np.float32) for r in res.results], 0)
```
